# Optimizing a Trainium2 kernel written in Bass

```python
import math
import jax, jax.numpy as jnp
from jax import lax
import numpy as np

D_MODEL = 1024
BATCH = 4
SEQ = 8192
DEPTH = 1

CHUNK = 64
EPS = 1e-6

A_HEADS = 8
A_HEAD_DIM = 64
A_WIDTH = A_HEADS * A_HEAD_DIM
A_LEFT_CHUNKS = 8
A_BAND = (A_LEFT_CHUNKS + 1) * CHUNK
A_REL_CLIP = 256

B_HEADS = 8
B_HEAD_DIM = 64
B_WIDTH = B_HEADS * B_HEAD_DIM
IDX_HEADS = 8
IDX_DIM = 64
TOPK_MAX = 256
Q_BLOCK = 128

T5_BUCKETS = 32
T5_MAX_DIST = 128

SPLITS = (A_WIDTH, A_WIDTH, A_WIDTH, A_WIDTH,
          B_WIDTH, B_HEAD_DIM, B_HEAD_DIM, B_WIDTH,
          IDX_HEADS * IDX_DIM, IDX_DIM, IDX_HEADS,
          D_MODEL, D_MODEL)
IN_COLS = 4 * A_WIDTH + 2 * B_WIDTH + 2 * B_HEAD_DIM + IDX_HEADS * IDX_DIM + IDX_DIM + IDX_HEADS + 2 * D_MODEL

kernel_name = "hybrid_chunkband_dsa_gated_merge"

NEG = -1e30


def rms_norm(x, g):
    xf = x.astype(jnp.float32)
    y = xf * lax.rsqrt(jnp.mean(xf * xf, axis=-1, keepdims=True) + EPS)
    return (y * g.astype(jnp.float32)).astype(x.dtype)


def t5_bucket(rel):
    half = T5_BUCKETS // 2
    max_exact = half // 2
    ret = jnp.where(rel > 0, half, 0)
    n = jnp.abs(rel)
    nf = jnp.maximum(n, 1).astype(jnp.float32)
    large = max_exact + (jnp.log(nf / max_exact) / math.log(T5_MAX_DIST / max_exact)
                         * (half - max_exact)).astype(jnp.int32)
    large = jnp.minimum(large, half - 1)
    return ret + jnp.where(n < max_exact, n, large)


def chunk_band_attention(q, k, v, rel_bias):
    B, S, H, Dh = q.shape
    nc = S // CHUNK
    pad = A_LEFT_CHUNKS * CHUNK
    kp = jnp.pad(k, ((0, 0), (pad, 0), (0, 0), (0, 0))).astype(jnp.float32)
    vp = jnp.pad(v, ((0, 0), (pad, 0), (0, 0), (0, 0))).astype(jnp.float32)
    qc = q.astype(jnp.float32).reshape(B, nc, CHUNK, H, Dh).swapaxes(0, 1)
    i = jnp.arange(CHUNK)
    j = jnp.arange(A_BAND)
    rel = i[:, None] - (j[None, :] - pad)
    bias = rel_bias.astype(jnp.float32)[:, jnp.clip(rel, -A_REL_CLIP, A_REL_CLIP) + A_REL_CLIP]
    scale = Dh ** -0.5

    def one_chunk(args):
        qb, c = args
        kb = lax.dynamic_slice_in_dim(kp, c * CHUNK, A_BAND, axis=1)
        vb = lax.dynamic_slice_in_dim(vp, c * CHUNK, A_BAND, axis=1)
        s = jnp.einsum('bqhd,bkhd->bhqk', qb, kb) * scale + bias[None]
        valid = (c * CHUNK - pad + j) >= 0
        s = jnp.where(valid[None, None, None, :], s, NEG)
        p = jax.nn.softmax(s, axis=-1)
        return jnp.einsum('bhqk,bkhd->bqhd', p, vb)

    out = lax.map(one_chunk, (qc, jnp.arange(nc)))
    return out.swapaxes(0, 1).reshape(B, S, H * Dh).astype(q.dtype)


def dsa_attention(q, k, v, qi, ki, wi, t5_bias):
    B, S, H, Dh = q.shape
    n_sel = min(TOPK_MAX, S // 4)
    nb = S // Q_BLOCK
    key_chunk = jnp.arange(S) // CHUNK
    k32 = k.astype(jnp.float32)
    v32 = v.astype(jnp.float32)
    ki32 = ki.astype(jnp.float32)
    tb = t5_bias.astype(jnp.float32)
    scale = Dh ** -0.5
    idx_scale = (IDX_HEADS ** -0.5) * (IDX_DIM ** -0.5)

    def blocks(a):
        return a.reshape((B, nb, Q_BLOCK) + a.shape[2:]).swapaxes(0, 1)

    gather = jax.vmap(lambda kk, ss: kk[ss])

    def one_block(args):
        qb, qib, wib, blk = args
        qpos = blk * Q_BLOCK + jnp.arange(Q_BLOCK)
        qchunk = qpos // CHUNK
        admissible = key_chunk[None, :] <= qchunk[:, None]
        idx_logits = jnp.einsum('bqhd,bsd->bqhs', qib.astype(jnp.float32), ki32)
        score = jnp.einsum('bqhs,bqh->bqs', jax.nn.relu(idx_logits),
                           wib.astype(jnp.float32) * idx_scale)
        score = jnp.where(admissible[None], score, -jnp.inf)
        _, sel = lax.top_k(score, n_sel)
        sel_ok = (sel // CHUNK) <= qchunk[None, :, None]
        ks = gather(k32, sel)
        vs = gather(v32, sel)
        s = jnp.einsum('bqhd,bqkd->bqhk', qb.astype(jnp.float32), ks) * scale
        bias = tb[t5_bucket(sel - qpos[None, :, None])]
        s = s + jnp.swapaxes(bias, 2, 3)
        s = jnp.where(sel_ok[:, :, None, :], s, NEG)
        p = jax.nn.softmax(s, axis=-1)
        return jnp.einsum('bqhk,bqkd->bqhd', p, vs)

    out = lax.map(one_block, (blocks(q), blocks(qi), blocks(wi), jnp.arange(nb)))
    return out.swapaxes(0, 1).reshape(B, S, H * Dh).astype(q.dtype)


def setup_inputs(seed: int = 0) -> dict:
    key = jax.random.key(seed)
    ks = jax.random.split(key, 10)
    f32 = jnp.float32
    x = jax.random.normal(ks[0], (BATCH, SEQ, D_MODEL), f32)
    norm_gain = 1.0 + 0.01 * jax.random.normal(ks[1], (DEPTH, D_MODEL), f32)
    w_in = jax.random.normal(ks[2], (DEPTH, D_MODEL, IN_COLS), f32) * D_MODEL ** -0.5
    a_rel_bias = 0.1 * jax.random.normal(ks[3], (DEPTH, A_HEADS, 2 * A_REL_CLIP + 1), f32)
    t5_bias = 0.1 * jax.random.normal(ks[4], (T5_BUCKETS, B_HEADS), f32)
    w_a_out = jax.random.normal(ks[5], (DEPTH, A_WIDTH, D_MODEL), f32) * A_WIDTH ** -0.5
    w_b_out = jax.random.normal(ks[6], (DEPTH, B_WIDTH, D_MODEL), f32) * B_WIDTH ** -0.5
    w_out = jax.random.normal(ks[7], (DEPTH, D_MODEL, D_MODEL), f32) * D_MODEL ** -0.5
    final_gain = 1.0 + 0.01 * jax.random.normal(ks[8], (D_MODEL,), f32)
    return {"x": x, "norm_gain": norm_gain, "w_in": w_in, "a_rel_bias": a_rel_bias,
            "t5_bias": t5_bias, "w_a_out": w_a_out, "w_b_out": w_b_out,
            "w_out": w_out, "final_gain": final_gain}


def reference(x, norm_gain, w_in, a_rel_bias, t5_bias, w_a_out, w_b_out, w_out, final_gain):
    B, S, _ = x.shape
    offsets = [int(o) for o in np.cumsum(SPLITS)[:-1]]
    h = x
    for l in range(DEPTH):
        hn = rms_norm(h, norm_gain[l])
        proj = hn @ w_in[l]
        (qa, ka, va, za, qb, kb, vb, zb, qi, ki, wi, ga, gb) = jnp.split(proj, offsets, axis=-1)
        ya = chunk_band_attention(qa.reshape(B, S, A_HEADS, A_HEAD_DIM),
                                  ka.reshape(B, S, A_HEADS, A_HEAD_DIM),
                                  va.reshape(B, S, A_HEADS, A_HEAD_DIM),
                                  a_rel_bias[l]) * jax.nn.silu(za)
        yb = dsa_attention(qb.reshape(B, S, B_HEADS, B_HEAD_DIM), kb, vb,
                           qi.reshape(B, S, IDX_HEADS, IDX_DIM), ki, wi,
                           t5_bias) * jax.nn.silu(zb)
        merged = jax.nn.sigmoid(ga) * (ya @ w_a_out[l]) + jax.nn.sigmoid(gb) * (yb @ w_b_out[l])
        h = h + merged @ w_out[l]
    return rms_norm(h, final_gain)
```

```python
import math
from contextlib import ExitStack

import numpy as np
import concourse.bass as bass
import concourse.mybir as mybir
from concourse.bass_utils import run_bass_kernel_spmd

F32 = mybir.dt.float32
BF16 = mybir.dt.bfloat16
ALU = mybir.AluOpType
AF = mybir.ActivationFunctionType

D = 1024
EPS = 1e-6
NBIS = 18
BIS_M = 16.0


class Buf:
    __slots__ = ("name", "last_write", "readers")

    def __init__(self, name):
        self.name = name
        self.last_write = None
        self.readers = {}


class _Eng:
    def __init__(self, name):
        self.name = name
        self.count = 0
        self.ops = []
        self.waited = {}


class Sched:
    N_DMA_SLOTS = 12

    def __init__(self):
        self.engs = {k: _Eng(k) for k in ("pe", "act", "dve", "pool", "sp")}
        self.dma_slot_count = {}
        self.dma_next = {k: 0 for k in self.engs}
        self.semkeys = set()

    def _deps(self, E, reads, writes):
        deps = {}

        def add(tok):
            if tok is None:
                return
            k, v = tok
            if deps.get(k, 0) < v:
                deps[k] = v

        for b in reads:
            add(b.last_write)
        for b in writes:
            add(b.last_write)
            for k, v in b.readers.items():
                add((k, v))
        out = []
        for k, v in deps.items():
            if k == ("eng", E.name) and E.name in ("pe", "sp"):
                continue
            if E.waited.get(k, 0) >= v:
                continue
            E.waited[k] = v
            out.append((k, v))
        return out

    def _commit(self, tok, reads, writes):
        k, v = tok
        for b in writes:
            b.last_write = tok
            b.readers = {}
        for b in reads:
            if b.readers.get(k, 0) < v:
                b.readers[k] = v

    def op(self, eng, fn, reads=(), writes=()):
        E = self.engs[eng]
        waits = self._deps(E, reads, writes)
        E.count += 1
        key = ("eng", E.name)
        self.semkeys.add(key)

        def emit(e, sems, waits=waits, fn=fn, key=key):
            for (k, v) in waits:
                e.wait_ge(sems[k], v)
            fn(e).then_inc(sems[key], 1)

        E.ops.append(emit)
        self._commit((key, E.count), reads, writes)

    def dma(self, queue, fn, reads=(), writes=()):
        E = self.engs[queue]
        slot = self.dma_next[queue] % self.N_DMA_SLOTS
        self.dma_next[queue] += 1
        key = ("dma", queue, slot)
        self.semkeys.add(key)
        n = self.dma_slot_count.get(key, 0)
        waits = self._deps(E, reads, writes)
        if n > 0 and E.waited.get(key, 0) < 16 * n:
            E.waited[key] = 16 * n
            waits.append((key, 16 * n))
        self.dma_slot_count[key] = n + 1

        def emit(e, sems, waits=waits, fn=fn, key=key):
            for (k, v) in waits:
                e.wait_ge(sems[k], v)
            fn(e).then_inc(sems[key], 16)

        E.ops.append(emit)
        self._commit((key, 16 * (n + 1)), reads, writes)

    def final_wait(self, eng, bufs):
        E = self.engs[eng]
        waits = self._deps(E, bufs, ())

        def emit(e, sems, waits=waits):
            for (k, v) in waits:
                e.wait_ge(sems[k], v)

        E.ops.append(emit)

    def emit(self, nc):
        with ExitStack() as st:
            sems = {}
            for i, k in enumerate(sorted(self.semkeys, key=str)):
                sems[k] = st.enter_context(nc.semaphore("s%d" % i))
            block = st.enter_context(nc.Block())
            engs = self.engs

            @block.tensor
            def _(e):
                for f in engs["pe"].ops:
                    f(e, sems)

            @block.scalar
            def _(e):
                for f in engs["act"].ops:
                    f(e, sems)

            @block.vector
            def _(e):
                for f in engs["dve"].ops:
                    f(e, sems)

            @block.gpsimd
            def _(e):
                for f in engs["pool"].ops:
                    f(e, sems)

            @block.sync
            def _(e):
                for f in engs["sp"].ops:
                    f(e, sems)


def build(S, nsel):
    NT = S // 128
    NS = S // 256
    nc = bass.Bass("TRN2", target_bir_lowering=False)

    def dram(name, shape, dt=F32, kind="ExternalInput"):
        return nc.dram_tensor(name, shape, dt, kind=kind).ap()

    xf = dram("xf", [S, D])
    xo = dram("xo", [S // 2, D])
    wkfm = dram("wkfm", [D, 640])
    wktm = dram("wktm", [D, 576])
    wqfm = dram("wqfm", [D, 1536])
    wqtm = dram("wqtm", [D, 3072])
    wwi = dram("wwi", [D, 8])
    wao = dram("wao", [512, D])
    wbo = dram("wbo", [512, D])
    wo = dram("wo", [D, D])
    gain = dram("gain", [128, D])
    fgain = dram("fgain", [128, D])
    identd = dram("ident", [128, 128])
    biasAd = dram("biasA", [128, 6 * 8 * 128])
    maskAd = dram("maskA", [128, 6 * 128])
    biasTd = dram("biasT", [128, 3 * 8 * 128])
    cTd = dram("cT", [128, 8])
    admBd = dram("admB", [128, 256])
    dseld = dram("dsel", [128, 128])
    outd = dram("out", [S // 2, D], kind="ExternalOutput")
    scr = [dram("scr%d" % i, [128, 4096], BF16, kind="Internal") for i in range(16)]
    b_scr = [Buf("scr%d" % i) for i in range(16)]

    sch = Sched()
    st = ExitStack()

    def sb(name, shape, dt):
        return st.enter_context(nc.sbuf_tensor(name, shape, dt)), Buf(name)

    def ps(name, shape, dt):
        return st.enter_context(nc.psum_tensor(name, shape, dt)), Buf(name)

    identb, b_identb = sb("identb", [128, 128], BF16)
    gain_t, b_gain = sb("gain_t", [128, D], F32)
    fgain_t, b_fgain = sb("fgain_t", [128, D], F32)
    Wwi, b_Wwi = sb("Wwi", [128, 8, 8], BF16)
    ring = [sb("ring%d" % i, [128, 4096], BF16) for i in range(2)]
    scores, b_scores = sb("scores", [128, max(S, 8192)], F32)
    maskf, b_maskf = sb("maskf", [128, S], BF16)
    kkT, b_kkT = sb("kkT", [128, S], BF16)
    vB, b_vB = sb("vB", [128, NT, 66], BF16)
    kaT, b_kaT = sb("kaT", [128, 4, 1024], BF16)
    vA, b_vA = sb("vA", [128, 8, 8, 66], BF16)
    biasAm, b_biasAm = sb("biasAm", [128, 6, 8, 128], BF16)
    biasTn, b_biasTn = sb("biasTn", [128, 3, 8, 128], BF16)
    admB, b_admB = sb("admB_t", [128, 256], F32)
    dsel, b_dsel = sb("dsel_t", [128, 8, 16], F32)
    neghalf, b_neghalf = sb("neghalf", [128, 1], F32)
    xin = [sb("xin%d" % i, [128, D], F32) for i in range(1)] * 2
    xown, b_xown = sb("xown", [128, D], F32)
    sqj, b_sqj = sb("sqj", [128, D], BF16)
    hn, b_hn = sb("hn", [128, D], BF16)
    hnT, b_hnT = sb("hnT", [128, 8, 384], BF16)
    stat, b_stat = sb("stat", [128, 8], F32)
    qaZ = [sb("qaZ%d" % i, [128, 4, 128], BF16) for i in range(2)]
    qbZ, b_qbZ = sb("qbZ", [128, 8, 128], BF16)
    qiZ, b_qiZ = sb("qiZ", [128, 8, 8, 16], BF16)
    tmpa, b_tmpa = sb("tmpa", [128, 512], F32)
    tmpb, b_tmpb = tmpa, b_tmpa
    sza, b_sza = sb("sza", [128, 512], F32)
    szb, b_szb = sb("szb", [128, 512], F32)
    sga, b_sga = sb("sga", [128, D], F32)
    sgb, b_sgb = sb("sgb", [128, D], F32)
    wsc, b_wsc = sb("wsc", [128, 8], F32)
    Tb, b_Tb = sb("Tb", [128, 8, 8, 16], BF16)
    Wblk, b_Wblk = sb("Wblk", [128, 8, 128], BF16)
    Rr = [sb("R%d" % i, [128, 512], BF16) for i in range(3)]
    PA, b_PA = sb("PA", [128, 6, 128], BF16)
    PT = [sb("PT%d" % i, [128, 8, 128], BF16) for i in range(2)]
    PTm = [sb("PTm%d" % i, [128, 8, 128], BF16) for i in range(2)]
    cand = [sb("cand%d" % i, [128, 1], F32) for i in range(2)]
    cnt, b_cnt = sb("cnt", [128, 1], F32)
    pmh, b_pmh = sb("pmh", [128, 1], F32)
    thr, b_thr = sb("thr", [128, 1], F32)
    rden, b_rden = sb("rden", [128, 8], F32)
    yun, b_yun = sb("yun", [128, 8, 64], F32)
    yag, b_yag = sb("yag", [128, 512], BF16)
    ybg, b_ybg = sb("ybg", [128, 512], BF16)
    yT, b_yT = sb("yT", [128, 4, 128], BF16)
    merged, b_merged = sb("merged", [128, D], F32)
    mergedb, b_mergedb = sb("mergedb", [128, D], BF16)
    mT, b_mT = sb("mT", [128, 8, 128], BF16)
    hres, b_hres = sb("hres", [128, D], F32)
    yout, b_yout = merged, b_merged
    pair = [ps("pair%d" % i, [128, 1024], F32) for i in range(3)]
    PB0, b_PB0 = ps("PB0", [128, 8, 128], BF16)
    PB1, b_PB1 = ps("PB1", [128, 8, 128], BF16)
    b_PB1s = [Buf("PB1s%d" % i) for i in range(4)]

    op = sch.op
    dma = sch.dma
    obufs = []

    dma("sp", lambda e: e.dma_start(out=gain_t[:], in_=gain[:, :]), writes=[b_gain])
    dma("sp", lambda e: e.dma_start(out=fgain_t[:], in_=fgain[:, :]), writes=[b_fgain])
    dma("sp", lambda e: e.dma_start(out=admB[:], in_=admBd[:, :]), writes=[b_admB])
    dma("sp", lambda e: e.dma_start(out=dsel[:], in_=dseld.rearrange("p (g q) -> p g q", g=8)), writes=[b_dsel])
    op("pool", lambda e: e.memset(neghalf[:], -0.5), writes=[b_neghalf])
    op("pool", lambda e: e.memset(vB[:], 1.0), writes=[b_vB])
    op("pool", lambda e: e.memset(vA[:], 1.0), writes=[b_vA])
    op("pool", lambda e: e.memset(kaT[:], 0.0), writes=[b_kaT])
    for t_, b_ in qaZ:
        op("pool", lambda e, t_=t_: e.memset(t_[:], 0.0), writes=[b_])
    op("pool", lambda e: e.memset(qbZ[:], 0.0), writes=[b_qbZ])
    op("pool", lambda e: e.memset(qiZ[:], 0.0), writes=[b_qiZ])

    stg = scores
    b_stg = b_scores
    op("pool", lambda e: e.memset(stg[:, 0:4096], 0.0), writes=[b_stg])
    dma("sp", lambda e: e.dma_start(out=stg[:, 0:128], in_=identd[:, :]), writes=[b_stg])
    op("dve", lambda e: e.tensor_copy(out=identb[:], in_=stg[:, 0:128]), reads=[b_stg], writes=[b_identb])
    dma("sp", lambda e: e.dma_start(out=stg[:, 0:6144], in_=biasAd[:, :]), writes=[b_stg])
    dma("sp", lambda e: e.dma_start(out=stg[:, 6144:6912], in_=maskAd[:, :]), writes=[b_stg])
    op("dve", lambda e: e.tensor_tensor(
        out=biasAm[:], in0=stg[:, 0:6144].rearrange("p (r h q) -> p r h q", r=6, h=8),
        in1=stg[:, 6144:6912].rearrange("p (r q) -> p r q", r=6).unsqueeze(2).to_broadcast([128, 6, 8, 128]),
        op=ALU.add), reads=[b_stg], writes=[b_biasAm])
    dma("sp", lambda e: e.dma_start(out=stg[:, 0:3072], in_=biasTd[:, :]), reads=[], writes=[b_stg])
    dma("sp", lambda e: e.dma_start(out=stg[:, 3072:3080], in_=cTd[:, :]), writes=[b_stg])
    op("dve", lambda e: e.tensor_tensor(
        out=biasTn[:], in0=stg[:, 0:3072].rearrange("p (r h q) -> p r h q", r=3, h=8),
        in1=stg[:, 3072:3080].unsqueeze(1).unsqueeze(3).to_broadcast([128, 3, 8, 128]),
        op=ALU.subtract), reads=[b_stg], writes=[b_biasTn])

    cvt = ring[0][0]
    b_cvt = ring[0][1]

    def cvt_chunk(si, pieces):
        for k in range(8):
            for (srcf, c0, ncl) in pieces:
                dma("sp", lambda e, k=k, srcf=srcf, c0=c0, ncl=ncl: e.dma_start(
                    out=stg[:, k * 512 + c0:k * 512 + c0 + ncl], in_=srcf(k)), writes=[b_stg])
        op("dve", lambda e: e.tensor_copy(out=cvt[:], in_=stg[:, 0:4096]), reads=[b_stg], writes=[b_cvt])
        dma("sp", lambda e, si=si: e.dma_start(out=scr[si][:, :], in_=cvt[:]), reads=[b_cvt], writes=[b_scr[si]])

    rows = lambda w_, k: slice(k * 128, (k + 1) * 128)
    cvt_chunk(0, [(lambda k: wkfm[k * 128:(k + 1) * 128, 0:512], 0, 512)])
    cvt_chunk(1, [(lambda k: wkfm[k * 128:(k + 1) * 128, 512:640], 0, 128),
                  (lambda k: wktm[k * 128:(k + 1) * 128, 512:576], 128, 64)])
    cvt_chunk(2, [(lambda k: wktm[k * 128:(k + 1) * 128, 0:512], 0, 512)])
    for g3 in range(3):
        cvt_chunk(3 + g3, [(lambda k, g3=g3: wqfm[k * 128:(k + 1) * 128, g3 * 512:(g3 + 1) * 512], 0, 512)])
    for ci in range(6):
        cvt_chunk(6 + ci, [(lambda k, ci=ci: wqtm[k * 128:(k + 1) * 128, ci * 512:(ci + 1) * 512], 0, 512)])
    for si, (src, r0) in enumerate([(wao, 0), (wbo, 0), (wo, 0), (wo, 512)]):
        for c4 in range(4):
            dma("sp", lambda e, c4=c4, src=src, r0=r0: e.dma_start(
                out=stg[:, c4 * 1024:(c4 + 1) * 1024], in_=src[r0 + c4 * 128:r0 + (c4 + 1) * 128, :]),
                writes=[b_stg])
        op("dve", lambda e: e.tensor_copy(out=cvt[:], in_=stg[:, 0:4096]), reads=[b_stg], writes=[b_cvt])
        dma("sp", lambda e, si=si: e.dma_start(out=scr[12 + si][:, :], in_=cvt[:]), reads=[b_cvt],
            writes=[b_scr[12 + si]])
    for k in range(8):
        dma("sp", lambda e, k=k: e.dma_start(out=stg[:, 0:8], in_=wwi[k * 128:(k + 1) * 128, :]), writes=[b_stg])
        op("dve", lambda e, k=k: e.tensor_copy(out=Wwi[:, k, :], in_=stg[:, 0:8]), reads=[b_stg], writes=[b_Wwi])

    ring_ctr = [0]

    def stream(si):
        r, b_r = ring[ring_ctr[0] % 2]
        ring_ctr[0] += 1
        dma("sp", lambda e: e.dma_start(out=r[:], in_=scr[si][:, :]), reads=[b_scr[si]], writes=[b_r])
        return r, b_r

    pair_ctr = [0]

    def next_pair():
        p = pair[pair_ctr[0] % 3]
        pair_ctr[0] += 1
        return p

    def rmsnorm_to(x_t, b_x, gain_tile, b_g, out_t, b_out):
        op("act", lambda e: e.activation(out=sqj[:], in_=x_t[:], func=AF.Square, accum_out=stat[:, 0:1]),
           reads=[b_x], writes=[b_sqj, b_stat])
        op("dve", lambda e: e.tensor_scalar(out=stat[:, 1:2], in0=stat[:, 0:1], scalar1=1.0 / D, scalar2=EPS,
                                            op0=ALU.mult, op1=ALU.add), reads=[b_stat], writes=[b_stat])
        op("pool", lambda e: e.tensor_tensor(out=stat[:, 2:3], in0=stat[:, 1:2], in1=neghalf[:], op=ALU.pow),
           reads=[b_stat, b_neghalf], writes=[b_stat])
        op("dve", lambda e: e.scalar_tensor_tensor(out=out_t[:], in0=x_t[:], scalar=stat[:, 2:3], in1=gain_tile[:],
                                                   op0=ALU.mult, op1=ALU.mult),
           reads=[b_x, b_stat, b_g], writes=[b_out])

    def norm_transpose(x_t, b_x, col):
        rmsnorm_to(x_t, b_x, gain_t, b_gain, hn, b_hn)
        for k in range(8):
            op("pe", lambda e, k=k: e.transpose(out=PB0[:, k, :], in_=hn[:, k * 128:(k + 1) * 128], identity=identb[:]),
               reads=[b_hn, b_identb], writes=[b_PB0])
        op("act", lambda e: e.activation(out=hnT[:, :, col:col + 128], in_=PB0[:], func=AF.Copy),
           reads=[b_PB0], writes=[b_hnT])

    idx_scale = (8 ** -0.5) * (64 ** -0.5)

    def slot(j):
        nk = 256 * (j + 1)
        pos0 = (2 * j) % 8
        for i in range(2):
            xt, b_xt = xin[i]
            t = 2 * j + i
            dma("sp", lambda e, xt=xt, t=t: e.dma_start(out=xt[:], in_=xf[t * 128:(t + 1) * 128, :]), writes=[b_xt])
            norm_transpose(xt, b_xt, i * 128)
        dma("sp", lambda e: e.dma_start(out=xown[:], in_=xo[j * 128:(j + 1) * 128, :]), writes=[b_xown])
        norm_transpose(xown, b_xown, 256)

        rA, b_rA = stream(0)
        for blk in range(4):
            pr, b_pr = next_pair()
            for k in range(8):
                op("pe", lambda e, k=k, blk=blk, pr=pr, rA=rA: e.matmul(
                    pr[:, 0:256], lhsT=rA[:, k * 512 + blk * 128:k * 512 + (blk + 1) * 128], rhs=hnT[:, k, 0:256],
                    start=(k == 0), stop=(k == 7)), reads=[b_rA, b_hnT], writes=[b_pr])
            op("act", lambda e, blk=blk, pr=pr: e.activation(
                out=kaT[:, blk, pos0 * 128:pos0 * 128 + 256], in_=pr[:, 0:256], func=AF.Copy),
                reads=[b_pr], writes=[b_kaT])
        rB, b_rB = stream(1)
        pr, b_pr = next_pair()
        for k in range(8):
            op("pe", lambda e, k=k, pr=pr, rB=rB: e.matmul(
                pr[:, 0:256], lhsT=rB[:, k * 512:k * 512 + 128], rhs=hnT[:, k, 0:256],
                start=(k == 0), stop=(k == 7)), reads=[b_rB, b_hnT], writes=[b_pr])
        op("act", lambda e, pr=pr: e.activation(
            out=kkT[:, 2 * j * 128:2 * j * 128 + 256], in_=pr[:, 0:256], func=AF.Copy),
            reads=[b_pr], writes=[b_kkT])
        for i in range(2):
            pr, b_pr = next_pair()
            for k in range(8):
                op("pe", lambda e, k=k, i=i, pr=pr, rB=rB: e.matmul(
                    pr[:, 0:64], lhsT=hnT[:, k, i * 128:(i + 1) * 128], rhs=rB[:, k * 512 + 128:k * 512 + 192],
                    start=(k == 0), stop=(k == 7)), reads=[b_rB, b_hnT], writes=[b_pr])
            op("dve", lambda e, i=i, pr=pr: e.tensor_copy(out=vB[:, 2 * j + i, 0:64], in_=pr[:, 0:64]),
               reads=[b_pr], writes=[b_vB])
        rC, b_rC = stream(2)
        for i in range(2):
            pr, b_pr = next_pair()
            for k in range(8):
                op("pe", lambda e, k=k, i=i, pr=pr, rC=rC: e.matmul(
                    pr[:, 0:512], lhsT=hnT[:, k, i * 128:(i + 1) * 128], rhs=rC[:, k * 512:(k + 1) * 512],
                    start=(k == 0), stop=(k == 7)), reads=[b_rC, b_hnT], writes=[b_pr])
            op("dve", lambda e, i=i, pr=pr: e.tensor_copy(
                out=vA[:, pos0 + i, :, 0:64], in_=pr[:, 0:512].rearrange("p (h d) -> p h d", h=8)),
                reads=[b_pr], writes=[b_vA])

        for grp in range(3):
            rQ, b_rQ = stream(3 + grp)
            pr, b_pr = next_pair()
            for b4 in range(4):
                blk = grp * 4 + b4
                for k in range(8):
                    op("pe", lambda e, k=k, b4=b4, pr=pr, rQ=rQ: e.matmul(
                        pr[:, b4 * 128:(b4 + 1) * 128], lhsT=rQ[:, k * 512 + b4 * 128:k * 512 + (b4 + 1) * 128],
                        rhs=hnT[:, k, 256:384], start=(k == 0), stop=(k == 7)),
                        reads=[b_rQ, b_hnT], writes=[b_pr])
            src = lambda lo, hi, pr=pr: pr[lo:hi, 0:512].rearrange("p (b q) -> p b q", b=4)
            if grp == 0:
                op("act", lambda e, src=src: e.activation(out=qaZ[0][0][0:64, :, :], in_=src(0, 64), func=AF.Copy,
                                                          scale=0.125), reads=[b_pr], writes=[qaZ[0][1]])
                op("act", lambda e, src=src: e.activation(out=qaZ[1][0][64:128, :, :], in_=src(64, 128), func=AF.Copy,
                                                          scale=0.125), reads=[b_pr], writes=[qaZ[1][1]])
            else:
                h0 = (grp - 1) * 4
                op("act", lambda e, src=src, h0=h0: e.activation(out=qbZ[0:64, h0:h0 + 4, :], in_=src(0, 64),
                                                                 func=AF.Copy, scale=0.125),
                   reads=[b_pr], writes=[b_qbZ])
                op("act", lambda e, pr=pr, h0=h0: e.activation(
                    out=qiZ[64:128, :, h0:h0 + 4, :],
                    in_=pr[64:128, 0:512].rearrange("p (h g q) -> p g h q", h=4, g=8), func=AF.Copy),
                    reads=[b_pr], writes=[b_qiZ])
        pr, b_pr = next_pair()
        for k in range(8):
            op("pe", lambda e, k=k, pr=pr: e.matmul(pr[:, 0:8], lhsT=hnT[:, k, 256:384], rhs=Wwi[:, k, :],
                                                    start=(k == 0), stop=(k == 7)),
               reads=[b_Wwi, b_hnT], writes=[b_pr])
        op("act", lambda e, pr=pr: e.activation(out=wsc[:], in_=pr[:, 0:8], func=AF.Copy, scale=idx_scale),
           reads=[b_pr], writes=[b_wsc])
        op("dve", lambda e: e.tensor_tensor(
            out=Tb[:], in0=wsc[:].unsqueeze(1).unsqueeze(3).to_broadcast([128, 8, 8, 16]),
            in1=dsel[:].unsqueeze(2).to_broadcast([128, 8, 8, 16]), op=ALU.mult),
            reads=[b_wsc, b_dsel], writes=[b_Tb])
        for g in range(8):
            op("pe", lambda e, g=g: e.transpose(out=PB0[:, g, :], in_=Tb[:, g, :, :].rearrange("p h q -> p (h q)"),
                                                identity=identb[:]), reads=[b_Tb, b_identb], writes=[b_PB0])
        op("act", lambda e: e.activation(out=Wblk[:], in_=PB0[:], func=AF.Copy), reads=[b_PB0], writes=[b_Wblk])

        for ci in range(6):
            r, b_r = stream(6 + ci)
            pr, b_pr = next_pair()
            for k in range(8):
                op("pe", lambda e, k=k, r=r, pr=pr: e.matmul(
                    pr[:, 0:512], lhsT=hnT[:, k, 256:384], rhs=r[:, k * 512:(k + 1) * 512],
                    start=(k == 0), stop=(k == 7)), reads=[b_r, b_hnT], writes=[b_pr])
            op("act", lambda e, pr=pr: e.activation(out=tmpa[:], in_=pr[:, 0:512], func=AF.Tanh, scale=0.5),
               reads=[b_pr], writes=[b_tmpa])
            if ci < 2:
                dst, b_dst = (sza, b_sza) if ci == 0 else (szb, b_szb)
                op("dve", lambda e: e.tensor_scalar(out=tmpb[:], in0=tmpa[:], scalar1=0.5, scalar2=0.5,
                                                    op0=ALU.mult, op1=ALU.add), reads=[b_tmpa], writes=[b_tmpb])
                op("dve", lambda e, dst=dst, pr=pr: e.tensor_tensor(out=dst[:], in0=tmpb[:], in1=pr[:, 0:512],
                                                                    op=ALU.mult),
                   reads=[b_tmpb, b_pr], writes=[b_dst])
            else:
                dst, b_dst = (sga, b_sga) if ci < 4 else (sgb, b_sgb)
                c0 = (ci % 2) * 512
                op("dve", lambda e, dst=dst, c0=c0: e.tensor_scalar(
                    out=dst[:, c0:c0 + 512], in0=tmpa[:], scalar1=0.5, scalar2=0.5, op0=ALU.mult, op1=ALU.add),
                    reads=[b_tmpa], writes=[b_dst])

        rvalid = [r for r in range(6) if 2 * j - 4 + r >= 0]
        pv, b_pv = pair[2]
        for h in range(8):
            p, e2 = h // 2, h % 2
            sp_, b_sp = pair[h % 2]
            for idx, r in enumerate(rvalid):
                kt = 2 * j - 4 + r
                kp = kt % 8
                op("pe", lambda e, idx=idx, p=p, kp=kp, e2=e2, sp_=sp_: e.matmul(
                    sp_[:, idx * 128:(idx + 1) * 128], lhsT=kaT[:, p, kp * 128:(kp + 1) * 128],
                    rhs=qaZ[e2][0][:, p, :], start=True, stop=False),
                    reads=[b_kaT, qaZ[e2][1]], writes=[b_sp])
                op("pe", lambda e, idx=idx, r=r, h=h, sp_=sp_: e.matmul(
                    sp_[:, idx * 128:(idx + 1) * 128], lhsT=identb[:], rhs=biasAm[:, r, h, :],
                    start=False, stop=True), reads=[b_identb, b_biasAm], writes=[b_sp])
            nr = len(rvalid)
            op("act", lambda e, nr=nr, sp_=sp_: e.activation(
                out=PA[:, 0:nr, :].rearrange("p r q -> p (r q)"), in_=sp_[:, 0:nr * 128], func=AF.Exp),
                reads=[b_sp], writes=[b_PA])
            for idx, r in enumerate(rvalid):
                kp = (2 * j - 4 + r) % 8
                op("pe", lambda e, idx=idx, kp=kp, h=h, nr=nr, pv=pv: e.matmul(
                    pv[:, h * 128:h * 128 + 65], lhsT=PA[:, idx, :], rhs=vA[:, kp, h, 0:65],
                    start=(idx == 0), stop=(idx == nr - 1)), reads=[b_PA, b_vA], writes=[b_pv])

        def finish_branch(pv, b_pv, sz, b_sz, yg, b_yg):
            pv3 = pv[:, :].rearrange("p (h c) -> p h c", h=8)
            op("dve", lambda e: e.reciprocal(out=rden[:], in_=pv3[:, :, 64]), reads=[b_pv], writes=[b_rden])
            op("dve", lambda e: e.tensor_tensor(out=yun[:], in0=pv3[:, :, 0:64],
                                                in1=rden[:].unsqueeze(2).to_broadcast([128, 8, 64]), op=ALU.mult),
               reads=[b_pv, b_rden], writes=[b_yun])
            op("dve", lambda e: e.tensor_tensor(out=yg[:], in0=yun[:].rearrange("p h d -> p (h d)"), in1=sz[:],
                                                op=ALU.mult), reads=[b_yun, b_sz], writes=[b_yg])

        finish_branch(pv, b_pv, sza, b_sza, yag, b_yag)

        nblk = (nk + 511) // 512
        for blk in range(nblk):
            w = min(512, nk - 512 * blk)
            k0 = blk * 512
            sacc, b_sacc = pair[2]
            for g in range(8):
                lg, b_lg = pair[g % 2]
                op("pe", lambda e, g=g, w=w, k0=k0, lg=lg: e.matmul(
                    lg[:, 0:w], lhsT=qiZ[:, g, :, :].rearrange("p h q -> p (h q)"), rhs=kkT[:, k0:k0 + w], start=True, stop=True),
                    reads=[b_qiZ, b_kkT], writes=[b_lg])
                R_, b_R = Rr[g % 3]
                op("act", lambda e, w=w, lg=lg, R_=R_: e.activation(out=R_[:, 0:w], in_=lg[:, 0:w], func=AF.Relu),
                   reads=[b_lg], writes=[b_R])
                op("pe", lambda e, g=g, w=w, sacc=sacc, R_=R_: e.matmul(
                    sacc[:, 0:w], lhsT=Wblk[:, g, :], rhs=R_[:, 0:w], start=(g == 0), stop=(g == 7)),
                    reads=[b_Wblk, b_R], writes=[b_sacc])
            last = (blk == nblk - 1)
            wcopy = w - 256 if last else w
            if wcopy > 0:
                op("act", lambda e, k0=k0, wcopy=wcopy, sacc=sacc: e.activation(
                    out=scores[:, k0:k0 + wcopy], in_=sacc[:, 0:wcopy], func=AF.Copy),
                    reads=[b_sacc], writes=[b_scores])
            if last:
                op("dve", lambda e, w=w, sacc=sacc: e.tensor_tensor(
                    out=scores[:, nk - 256:nk], in0=sacc[:, w - 256:w], in1=admB[:], op=ALU.add),
                    reads=[b_sacc, b_admB], writes=[b_scores])

        op("dve", lambda e: e.memset(cand[0][0][:], 0.0), writes=[cand[0][1]])
        step = BIS_M
        for it in range(NBIS):
            ca, b_ca = cand[it % 2]
            cb, b_cb = cand[(it + 1) % 2]
            op("dve", lambda e, ca=ca: e.tensor_scalar(
                out=maskf[:, 0:nk], in0=scores[:, 0:nk], scalar1=ca[:, 0:1], scalar2=None, op0=ALU.is_ge,
                op1=ALU.add, accum_out=cnt[:, 0:1]), reads=[b_scores, b_ca], writes=[b_maskf, b_cnt])
            op("dve", lambda e: e.tensor_scalar(out=pmh[:], in0=cnt[:], scalar1=float(nsel), scalar2=0.5,
                                                op0=ALU.is_ge, op1=ALU.subtract), reads=[b_cnt], writes=[b_pmh])
            op("dve", lambda e, ca=ca, cb=cb, step=step: e.scalar_tensor_tensor(
                out=cb[:], in0=pmh[:], scalar=float(step), in1=ca[:], op0=ALU.mult, op1=ALU.add),
                reads=[b_pmh, b_ca], writes=[b_cb])
            step = step / 2.0
        cf, b_cf = cand[NBIS % 2]
        op("dve", lambda e, cf=cf, step=step: e.tensor_scalar(out=thr[:], in0=cf[:], scalar1=-float(step),
                                                              scalar2=None, op0=ALU.add),
           reads=[b_cf], writes=[b_thr])
        op("dve", lambda e: e.tensor_scalar(out=maskf[:, 0:nk], in0=scores[:, 0:nk], scalar1=thr[:, 0:1],
                                            scalar2=None, op0=ALU.is_ge),
           reads=[b_scores, b_thr], writes=[b_maskf])

        pvb, b_pvb = pair[2]
        nkt = 2 * j + 2
        for kt in range(nkt):
            s4 = kt % 4
            op("pe", lambda e, kt=kt, s4=s4: e.transpose(out=PB1[:, s4, :], in_=maskf[:, kt * 128:(kt + 1) * 128],
                                                         identity=identb[:]),
               reads=[b_maskf, b_identb], writes=[b_PB1s[s4]])
            sp_, b_sp = pair[kt % 2]
            near = kt >= 2 * j - 1
            for half in range(2):
                op("pe", lambda e, kt=kt, half=half, near=near, sp_=sp_: e.matmul(
                    sp_[:, half * 512:(half + 1) * 512], lhsT=kkT[:, kt * 128:(kt + 1) * 128],
                    rhs=qbZ[:, 4 * half:4 * half + 4, :].rearrange("p h q -> p (h q)"), start=True, stop=(not near)),
                    reads=[b_kkT, b_qbZ], writes=[b_sp])
                if near:
                    r = kt - (2 * j - 1)
                    op("pe", lambda e, half=half, r=r, sp_=sp_: e.matmul(
                        sp_[:, half * 512:(half + 1) * 512], lhsT=identb[:],
                        rhs=biasTn[:, r, 4 * half:4 * half + 4, :].rearrange("p h q -> p (h q)"), start=False, stop=True),
                        reads=[b_identb, b_biasTn], writes=[b_sp])
            pt, b_pt = PT[kt % 2]
            ptm, b_ptm = PTm[kt % 2]
            op("act", lambda e, sp_=sp_, pt=pt: e.activation(out=pt[:].rearrange("p h q -> p (h q)"),
                                                             in_=sp_[:, :], func=AF.Exp),
               reads=[b_sp], writes=[b_pt])
            op("dve", lambda e, s4=s4, pt=pt, ptm=ptm: e.tensor_tensor(
                out=ptm[:], in0=pt[:], in1=PB1[:, s4, :].unsqueeze(1).to_broadcast([128, 8, 128]), op=ALU.mult),
                reads=[b_pt, b_PB1s[s4]], writes=[b_ptm])
            for h in range(8):
                op("pe", lambda e, kt=kt, h=h, ptm=ptm: e.matmul(
                    pvb[:, h * 128:h * 128 + 65], lhsT=ptm[:, h, :], rhs=vB[:, kt, 0:65],
                    start=(kt == 0 and h % 4 == 0), stop=(kt == nkt - 1), skip_group_check=True),
                    reads=[b_ptm, b_vB], writes=[b_pvb])
        finish_branch(pvb, b_pvb, szb, b_szb, ybg, b_ybg)
        pair_ctr[0] = 0

        def out_proj_branch(yg, b_yg, si, sg, b_sg, first):
            for c4 in range(4):
                op("pe", lambda e, c4=c4: e.transpose(out=PB0[:, c4, :], in_=yg[:, c4 * 128:(c4 + 1) * 128],
                                                      identity=identb[:]), reads=[b_yg, b_identb], writes=[b_PB0])
            op("act", lambda e: e.activation(out=yT[:], in_=PB0[:, 0:4, :], func=AF.Copy), reads=[b_PB0], writes=[b_yT])
            r, b_r = stream(si)
            pr, b_pr = next_pair()
            for half in range(2):
                for c4 in range(4):
                    op("pe", lambda e, c4=c4, half=half, r=r, pr=pr: e.matmul(
                        pr[:, half * 512:(half + 1) * 512], lhsT=yT[:, c4, :],
                        rhs=r[:, c4 * 1024 + half * 512:c4 * 1024 + (half + 1) * 512],
                        start=(c4 == 0), stop=(c4 == 3)), reads=[b_yT, b_r], writes=[b_pr])
            if first:
                op("dve", lambda e, pr=pr: e.tensor_tensor(out=merged[:], in0=pr[:, :], in1=sg[:], op=ALU.mult),
                   reads=[b_pr, b_sg], writes=[b_merged])
            else:
                op("dve", lambda e, pr=pr: e.tensor_tensor(out=hres[:], in0=pr[:, :], in1=sg[:], op=ALU.mult),
                   reads=[b_pr, b_sg], writes=[b_hres])
                op("dve", lambda e: e.tensor_tensor(out=mergedb[:], in0=merged[:], in1=hres[:], op=ALU.add),
                   reads=[b_merged, b_hres], writes=[b_mergedb])

        out_proj_branch(yag, b_yag, 12, sga, b_sga, True)
        out_proj_branch(ybg, b_ybg, 13, sgb, b_sgb, False)
        for k in range(8):
            op("pe", lambda e, k=k: e.transpose(out=PB0[:, k, :], in_=mergedb[:, k * 128:(k + 1) * 128],
                                                identity=identb[:]), reads=[b_mergedb, b_identb], writes=[b_PB0])
        op("act", lambda e: e.activation(out=mT[:], in_=PB0[:], func=AF.Copy), reads=[b_PB0], writes=[b_mT])
        pr, b_pr = next_pair()
        for wi_ in range(2):
            r, b_r = stream(14 + wi_)
            for half in range(2):
                for c4 in range(4):
                    k = wi_ * 4 + c4
                    op("pe", lambda e, c4=c4, k=k, half=half, r=r, pr=pr: e.matmul(
                        pr[:, half * 512:(half + 1) * 512], lhsT=mT[:, k, :],
                        rhs=r[:, c4 * 1024 + half * 512:c4 * 1024 + (half + 1) * 512],
                        start=(k == 0), stop=(k == 7), skip_group_check=True), reads=[b_mT, b_r], writes=[b_pr])
        op("dve", lambda e, pr=pr: e.tensor_tensor(out=hres[:], in0=pr[:, :], in1=xown[:], op=ALU.add),
           reads=[b_pr, b_xown], writes=[b_hres])
        rmsnorm_to(hres, b_hres, fgain_t, b_fgain, yout, b_yout)
        b_o = Buf("o%d" % j)
        dma("sp", lambda e: e.dma_start(out=outd[j * 128:(j + 1) * 128, :], in_=yout[:]), reads=[b_yout], writes=[b_o])
        obufs.append(b_o)
    for j_ in range(NS):
        slot(j_)
    sch.final_wait("sp", obufs)
    sch.emit(nc)
    st.close()
    return nc


def _t5_bucket(rel):
    half, max_exact = 16, 8
    ret = np.where(rel > 0, half, 0)
    n = np.abs(rel)
    nf = np.maximum(n, 1).astype(np.float32)
    large = max_exact + (np.log(nf / np.float32(max_exact)) / np.float32(math.log(128 / max_exact))
                         * np.float32(half - max_exact)).astype(np.int32)
    large = np.minimum(large, half - 1)
    return ret + np.where(n < max_exact, n, large)


def prep_shared(norm_gain, w_in, w_a_out, w_b_out, w_out, final_gain):
    W = np.asarray(w_in[0])
    offs = np.cumsum([0, 512, 512, 512, 512, 512, 64, 64, 512, 512, 64, 8, 1024, 1024])
    names = ["qa", "ka", "va", "za", "qb", "kb", "vb", "zb", "qi", "ki", "wi", "ga", "gb"]
    col = {n: np.arange(offs[i], offs[i + 1]) for i, n in enumerate(names)}
    qbqi = np.concatenate([np.concatenate([col["qb"][h * 64:(h + 1) * 64], col["qi"][h * 64:(h + 1) * 64]])
                           for h in range(8)])
    sh = {
        "wkfm": np.ascontiguousarray(W[:, np.concatenate([col["ka"], col["kb"], col["ki"]])]),
        "wktm": np.ascontiguousarray(W[:, np.concatenate([col["va"], col["vb"]])]),
        "wqfm": np.ascontiguousarray(W[:, np.concatenate([col["qa"], qbqi])]),
        "wqtm": np.ascontiguousarray(W[:, np.concatenate([col["za"], col["zb"], col["ga"], col["gb"]])]),
        "wwi": np.ascontiguousarray(W[:, col["wi"]]),
        "wao": np.ascontiguousarray(w_a_out[0]),
        "wbo": np.ascontiguousarray(w_b_out[0]),
        "wo": np.ascontiguousarray(w_out[0]),
        "gain": np.ascontiguousarray(np.broadcast_to(np.asarray(norm_gain[0])[None, :], (128, D))),
        "fgain": np.ascontiguousarray(np.broadcast_to(np.asarray(final_gain)[None, :], (128, D))),
        "ident": np.eye(128, dtype=np.float32),
    }
    ds = np.zeros((128, 8, 16), np.float32)
    for q in range(128):
        ds[q, q // 16, q % 16] = 1.0
    sh["dsel"] = ds.reshape(128, 128)
    return sh


def prep_core_consts(c, a_rel_bias, t5_bias):
    arb = np.asarray(a_rel_bias[0])
    tb = np.asarray(t5_bias)
    so = np.arange(128)[:, None]
    to = np.arange(128)[None, :]
    biasA = np.zeros((128, 6, 8, 128), np.float32)
    maskA = np.zeros((128, 6, 128), np.float32)
    for r in range(6):
        rel = (c + 4 - r) * 128 + to - so
        idx = np.clip(rel, -256, 256) + 256
        biasA[:, r, :, :] = np.transpose(arb[:, idx], (1, 0, 2))
        diff = 2 * c + 8 - 2 * r + (to >= 64).astype(int) - (so >= 64).astype(int)
        maskA[:, r, :] = np.where((diff >= 0) & (diff <= 8), 0.0, -30000.0)
    biasT = np.zeros((128, 3, 8, 128), np.float32)
    for r in range(3):
        rel = (r - 1 - c) * 128 + so - to
        bk = _t5_bucket(rel.astype(np.int32))
        biasT[:, r, :, :] = np.transpose(tb[bk], (0, 2, 1))
    cT = np.ascontiguousarray(np.broadcast_to(tb[15][None, :], (128, 8)))
    adm = np.zeros((128, 256), np.float32)
    qh = (np.arange(128) >= 64).astype(int)[:, None]
    kc4 = (np.arange(256) // 64)[None, :]
    adm[:] = np.where(kc4 <= 2 * c + qh, 0.0, -1e30)
    return {"biasA": biasA.reshape(128, -1), "maskA": maskA.reshape(128, -1),
            "biasT": biasT.reshape(128, -1), "cT": cT, "admB": adm}


def make_in_maps(x, norm_gain, w_in, a_rel_bias, t5_bias, w_a_out, w_b_out, w_out, final_gain):
    x = np.asarray(x)
    B, S, _ = x.shape
    sh = prep_shared(norm_gain, w_in, w_a_out, w_b_out, w_out, final_gain)
    cc = [prep_core_consts(c, a_rel_bias, t5_bias) for c in range(2)]
    maps = []
    for b in range(B):
        xb = x[b].reshape(S // 128, 128, D)
        for c in range(2):
            m = dict(sh)
            m.update(cc[c])
            m["xf"] = np.ascontiguousarray(x[b])
            m["xo"] = np.ascontiguousarray(xb[c::2].reshape(S // 2, D))
            maps.append(m)
    return maps


def assemble(results, B, S):
    out = np.zeros((B, S // 128, 128, D), np.float32)
    i = 0
    for b in range(B):
        for c in range(2):
            out[b, c::2] = np.asarray(results[i]["out"]).reshape(S // 256, 128, D)
            i += 1
    return out.reshape(B, S, D)


def kernel(x, norm_gain, w_in, a_rel_bias, t5_bias, w_a_out, w_b_out, w_out, final_gain):
    x = np.asarray(x)
    B, S, _ = x.shape
    nc = build(S, min(256, S // 4))
    maps = make_in_maps(x, norm_gain, w_in, a_rel_bias, t5_bias, w_a_out, w_b_out, w_out, final_gain)
    res = run_bass_kernel_spmd(nc, maps, core_ids=list(range(len(maps))))
    return assemble(res.results, B, S)
```

```python
import math
from contextlib import ExitStack

import numpy as np
import concourse.bass as bass
import concourse.mybir as mybir
from concourse.bass_utils import run_bass_kernel_spmd

F32 = mybir.dt.float32
BF16 = mybir.dt.bfloat16
ALU = mybir.AluOpType
AF = mybir.ActivationFunctionType

D = 1024
EPS = 1e-6
NBIS = 18
BIS_M = 16.0


class Buf:
    __slots__ = ("name", "last_write", "readers")

    def __init__(self, name):
        self.name = name
        self.last_write = None
        self.readers = {}


class _Eng:
    def __init__(self, name):
        self.name = name
        self.count = 0
        self.ops = []
        self.waited = {}


class Sched:
    N_DMA_SLOTS = 12

    def __init__(self):
        self.engs = {k: _Eng(k) for k in ("pe", "act", "dve", "pool", "sp")}
        self.dma_slot_count = {}
        self.dma_next = {k: 0 for k in self.engs}
        self.semkeys = set()

    def _deps(self, E, reads, writes):
        deps = {}

        def add(tok):
            if tok is None:
                return
            k, v = tok
            if deps.get(k, 0) < v:
                deps[k] = v

        for b in reads:
            add(b.last_write)
        for b in writes:
            add(b.last_write)
            for k, v in b.readers.items():
                add((k, v))
        out = []
        for k, v in deps.items():
            if k == ("eng", E.name) and E.name in ("pe", "sp"):
                continue
            if E.waited.get(k, 0) >= v:
                continue
            E.waited[k] = v
            out.append((k, v))
        return out

    def _commit(self, tok, reads, writes):
        k, v = tok
        for b in writes:
            b.last_write = tok
            b.readers = {}
        for b in reads:
            if b.readers.get(k, 0) < v:
                b.readers[k] = v

    def op(self, eng, fn, reads=(), writes=()):
        E = self.engs[eng]
        waits = self._deps(E, reads, writes)
        E.count += 1
        key = ("eng", E.name)
        self.semkeys.add(key)

        def emit(e, sems, waits=waits, fn=fn, key=key):
            for (k, v) in waits:
                e.wait_ge(sems[k], v)
            fn(e).then_inc(sems[key], 1)

        E.ops.append(emit)
        self._commit((key, E.count), reads, writes)

    def dma(self, queue, fn, reads=(), writes=()):
        E = self.engs[queue]
        slot = self.dma_next[queue] % self.N_DMA_SLOTS
        self.dma_next[queue] += 1
        key = ("dma", queue, slot)
        self.semkeys.add(key)
        n = self.dma_slot_count.get(key, 0)
        waits = self._deps(E, reads, writes)
        if n > 0 and E.waited.get(key, 0) < 16 * n:
            E.waited[key] = 16 * n
            waits.append((key, 16 * n))
        self.dma_slot_count[key] = n + 1

        def emit(e, sems, waits=waits, fn=fn, key=key):
            for (k, v) in waits:
                e.wait_ge(sems[k], v)
            fn(e).then_inc(sems[key], 16)

        E.ops.append(emit)
        self._commit((key, 16 * (n + 1)), reads, writes)

    def final_wait(self, eng, bufs):
        E = self.engs[eng]
        waits = self._deps(E, bufs, ())

        def emit(e, sems, waits=waits):
            for (k, v) in waits:
                e.wait_ge(sems[k], v)

        E.ops.append(emit)

    def emit(self, nc):
        with ExitStack() as st:
            sems = {}
            for i, k in enumerate(sorted(self.semkeys, key=str)):
                sems[k] = st.enter_context(nc.semaphore("s%d" % i))
            block = st.enter_context(nc.Block())
            engs = self.engs

            @block.tensor
            def _(e):
                for f in engs["pe"].ops:
                    f(e, sems)

            @block.scalar
            def _(e):
                for f in engs["act"].ops:
                    f(e, sems)

            @block.vector
            def _(e):
                for f in engs["dve"].ops:
                    f(e, sems)

            @block.gpsimd
            def _(e):
                for f in engs["pool"].ops:
                    f(e, sems)

            @block.sync
            def _(e):
                for f in engs["sp"].ops:
                    f(e, sems)


def build(S, nsel):
    NT = S // 128
    NS = S // 256
    nc = bass.Bass("TRN2", target_bir_lowering=False)

    def dram(name, shape, dt=F32, kind="ExternalInput"):
        return nc.dram_tensor(name, shape, dt, kind=kind).ap()

    xf = dram("xf", [S, D])
    xo = dram("xo", [S // 2, D])
    wkfm = dram("wkfm", [D, 640])
    wktm = dram("wktm", [D, 576])
    wqfm = dram("wqfm", [D, 1536])
    wqtm = dram("wqtm", [D, 3072])
    wwi = dram("wwi", [D, 8])
    wao = dram("wao", [512, D])
    wbo = dram("wbo", [512, D])
    wo = dram("wo", [D, D])
    gain = dram("gain", [128, D])
    fgain = dram("fgain", [128, D])
    identd = dram("ident", [128, 128])
    biasAd = dram("biasA", [128, 6 * 8 * 128])
    maskAd = dram("maskA", [128, 6 * 128])
    biasTd = dram("biasT", [128, 3 * 8 * 128])
    cTd = dram("cT", [128, 8])
    admBd = dram("admB", [128, 256])
    dseld = dram("dsel", [128, 128])
    outd = dram("out", [S // 2, D], kind="ExternalOutput")
    scr = [dram("scr%d" % i, [128, 4096], BF16, kind="Internal") for i in range(16)]
    b_scr = [Buf("scr%d" % i) for i in range(16)]

    sch = Sched()
    st = ExitStack()

    def sb(name, shape, dt):
        return st.enter_context(nc.sbuf_tensor(name, shape, dt)), Buf(name)

    def ps(name, shape, dt):
        return st.enter_context(nc.psum_tensor(name, shape, dt)), Buf(name)

    identb, b_identb = sb("identb", [128, 128], BF16)
    identf, b_identf = sb("identf", [128, 128], F32)
    gain_t, b_gain = sb("gain_t", [128, D], F32)
    fgain_t, b_fgain = sb("fgain_t", [128, D], F32)
    Wwi, b_Wwi = sb("Wwi", [128, 8, 8], BF16)
    ring = [sb("ring%d" % i, [128, 4096], BF16) for i in range(2)]
    scores, b_scores = sb("scores", [128, max(S, 8192)], F32)
    maskf, b_maskf = sb("maskf", [128, S], BF16)
    kkT, b_kkT = sb("kkT", [128, S], BF16)
    vB, b_vB = sb("vB", [128, NT, 66], BF16)
    kaT, b_kaT = sb("kaT", [128, 4, 1024], BF16)
    vA, b_vA = sb("vA", [128, 8, 8, 66], BF16)
    biasAm, b_biasAm = sb("biasAm", [128, 6, 8, 128], BF16)
    biasTn, b_biasTn = sb("biasTn", [128, 3, 8, 128], BF16)
    admB, b_admB = sb("admB_t", [128, 256], F32)
    dsel, b_dsel = sb("dsel_t", [128, 8, 16], F32)
    neghalf, b_neghalf = sb("neghalf", [128, 1], F32)
    xin = [sb("xin%d" % i, [128, D], F32) for i in range(1)] * 2
    xown, b_xown = sb("xown", [128, D], F32)
    sqj, b_sqj = sb("sqj", [128, D], BF16)
    hn, b_hn = sb("hn", [128, D], BF16)
    hnT, b_hnT = sb("hnT", [128, 8, 384], BF16)
    stat, b_stat = sb("stat", [128, 8], F32)
    qaZ = [sb("qaZ%d" % i, [128, 4, 128], BF16) for i in range(2)]
    qbZ, b_qbZ = sb("qbZ", [128, 8, 128], BF16)
    qiZ, b_qiZ = sb("qiZ", [128, 8, 8, 16], BF16)
    tmpa, b_tmpa = sb("tmpa", [128, 512], F32)
    tmpb, b_tmpb = tmpa, b_tmpa
    sza, b_sza = sb("sza", [128, 512], F32)
    szb, b_szb = sb("szb", [128, 512], F32)
    sga, b_sga = sb("sga", [128, D], F32)
    sgb, b_sgb = sb("sgb", [128, D], F32)
    wsc, b_wsc = sb("wsc", [128, 8], F32)
    Tb, b_Tb = sb("Tb", [128, 8, 8, 16], BF16)
    Wblk, b_Wblk = sb("Wblk", [128, 8, 128], BF16)
    Rr = [sb("R%d" % i, [128, 512], BF16) for i in range(3)]
    PAs = [sb("PA%d" % i, [128, 6, 128], BF16) for i in range(2)]
    PT = [sb("PT%d" % i, [128, 8, 128], BF16) for i in range(2)]
    PTm = [sb("PTm%d" % i, [128, 8, 128], BF16) for i in range(2)]
    cand = [sb("cand%d" % i, [128, 1], F32) for i in range(2)]
    cnt, b_cnt = sb("cnt", [128, 1], F32)
    pmh, b_pmh = sb("pmh", [128, 1], F32)
    thr, b_thr = sb("thr", [128, 1], F32)
    rden, b_rden = sb("rden", [128, 8], F32)
    yun, b_yun = sb("yun", [128, 8, 64], F32)
    yag, b_yag = sb("yag", [128, 512], BF16)
    ybg, b_ybg = sb("ybg", [128, 512], BF16)
    yT, b_yT = sb("yT", [128, 4, 128], BF16)
    merged, b_merged = sb("merged", [128, D], F32)
    mergedb, b_mergedb = sb("mergedb", [128, D], BF16)
    mT, b_mT = sb("mT", [128, 8, 128], BF16)
    hres, b_hres = sb("hres", [128, D], F32)
    yout, b_yout = merged, b_merged
    pair = [ps("pair%d" % i, [128, 1024], F32) for i in range(3)]
    PB0, b_PB0 = ps("PB0", [128, 8, 128], BF16)
    PB1, b_PB1 = ps("PB1", [128, 8, 128], BF16)
    b_PB1s = [Buf("PB1s%d" % i) for i in range(4)]

    op = sch.op
    dma = sch.dma
    obufs = []

    dma("sp", lambda e: e.dma_start(out=gain_t[:], in_=gain[:, :]), writes=[b_gain])
    dma("sp", lambda e: e.dma_start(out=fgain_t[:], in_=fgain[:, :]), writes=[b_fgain])
    dma("sp", lambda e: e.dma_start(out=admB[:], in_=admBd[:, :]), writes=[b_admB])
    dma("sp", lambda e: e.dma_start(out=dsel[:], in_=dseld.rearrange("p (g q) -> p g q", g=8)), writes=[b_dsel])
    op("pool", lambda e: e.memset(neghalf[:], -0.5), writes=[b_neghalf])
    op("pool", lambda e: e.memset(vB[:], 1.0), writes=[b_vB])
    op("pool", lambda e: e.memset(vA[:], 1.0), writes=[b_vA])
    op("pool", lambda e: e.memset(kaT[:], 0.0), writes=[b_kaT])
    for t_, b_ in qaZ:
        op("pool", lambda e, t_=t_: e.memset(t_[:], 0.0), writes=[b_])
    op("pool", lambda e: e.memset(qbZ[:], 0.0), writes=[b_qbZ])
    op("pool", lambda e: e.memset(qiZ[:], 0.0), writes=[b_qiZ])

    stg = scores
    b_stg = b_scores
    op("pool", lambda e: e.memset(stg[:, 0:4096], 0.0), writes=[b_stg])
    dma("sp", lambda e: e.dma_start(out=stg[:, 0:128], in_=identd[:, :]), writes=[b_stg])
    op("dve", lambda e: e.tensor_copy(out=identb[:], in_=stg[:, 0:128]), reads=[b_stg], writes=[b_identb])
    op("dve", lambda e: e.tensor_copy(out=identf[:], in_=stg[:, 0:128]), reads=[b_stg], writes=[b_identf])
    dma("sp", lambda e: e.dma_start(out=stg[:, 0:6144], in_=biasAd[:, :]), writes=[b_stg])
    dma("sp", lambda e: e.dma_start(out=stg[:, 6144:6912], in_=maskAd[:, :]), writes=[b_stg])
    op("dve", lambda e: e.tensor_tensor(
        out=biasAm[:], in0=stg[:, 0:6144].rearrange("p (r h q) -> p r h q", r=6, h=8),
        in1=stg[:, 6144:6912].rearrange("p (r q) -> p r q", r=6).unsqueeze(2).to_broadcast([128, 6, 8, 128]),
        op=ALU.add), reads=[b_stg], writes=[b_biasAm])
    dma("sp", lambda e: e.dma_start(out=stg[:, 0:3072], in_=biasTd[:, :]), reads=[], writes=[b_stg])
    dma("sp", lambda e: e.dma_start(out=stg[:, 3072:3080], in_=cTd[:, :]), writes=[b_stg])
    op("dve", lambda e: e.tensor_tensor(
        out=biasTn[:], in0=stg[:, 0:3072].rearrange("p (r h q) -> p r h q", r=3, h=8),
        in1=stg[:, 3072:3080].unsqueeze(1).unsqueeze(3).to_broadcast([128, 3, 8, 128]),
        op=ALU.subtract), reads=[b_stg], writes=[b_biasTn])

    cvt = ring[0][0]
    b_cvt = ring[0][1]

    def cvt_chunk(si, pieces):
        for k in range(8):
            for (srcf, c0, ncl) in pieces:
                dma("sp", lambda e, k=k, srcf=srcf, c0=c0, ncl=ncl: e.dma_start(
                    out=stg[:, k * 512 + c0:k * 512 + c0 + ncl], in_=srcf(k)), writes=[b_stg])
        op("dve", lambda e: e.tensor_copy(out=cvt[:], in_=stg[:, 0:4096]), reads=[b_stg], writes=[b_cvt])
        dma("sp", lambda e, si=si: e.dma_start(out=scr[si][:, :], in_=cvt[:]), reads=[b_cvt], writes=[b_scr[si]])

    rows = lambda w_, k: slice(k * 128, (k + 1) * 128)
    cvt_chunk(0, [(lambda k: wkfm[k * 128:(k + 1) * 128, 0:512], 0, 512)])
    cvt_chunk(1, [(lambda k: wkfm[k * 128:(k + 1) * 128, 512:640], 0, 128),
                  (lambda k: wktm[k * 128:(k + 1) * 128, 512:576], 128, 64)])
    cvt_chunk(2, [(lambda k: wktm[k * 128:(k + 1) * 128, 0:512], 0, 512)])
    for g3 in range(3):
        cvt_chunk(3 + g3, [(lambda k, g3=g3: wqfm[k * 128:(k + 1) * 128, g3 * 512:(g3 + 1) * 512], 0, 512)])
    for ci in range(6):
        cvt_chunk(6 + ci, [(lambda k, ci=ci: wqtm[k * 128:(k + 1) * 128, ci * 512:(ci + 1) * 512], 0, 512)])
    for si, (src, r0) in enumerate([(wao, 0), (wbo, 0), (wo, 0), (wo, 512)]):
        for c4 in range(4):
            dma("sp", lambda e, c4=c4, src=src, r0=r0: e.dma_start(
                out=stg[:, c4 * 1024:(c4 + 1) * 1024], in_=src[r0 + c4 * 128:r0 + (c4 + 1) * 128, :]),
                writes=[b_stg])
        op("dve", lambda e: e.tensor_copy(out=cvt[:], in_=stg[:, 0:4096]), reads=[b_stg], writes=[b_cvt])
        dma("sp", lambda e, si=si: e.dma_start(out=scr[12 + si][:, :], in_=cvt[:]), reads=[b_cvt],
            writes=[b_scr[12 + si]])
    for k in range(8):
        dma("sp", lambda e, k=k: e.dma_start(out=stg[:, 0:8], in_=wwi[k * 128:(k + 1) * 128, :]), writes=[b_stg])
        op("dve", lambda e, k=k: e.tensor_copy(out=Wwi[:, k, :], in_=stg[:, 0:8]), reads=[b_stg], writes=[b_Wwi])

    ring_ctr = [0]

    def stream(si):
        r, b_r = ring[ring_ctr[0] % 2]
        ring_ctr[0] += 1
        dma("sp", lambda e: e.dma_start(out=r[:], in_=scr[si][:, :]), reads=[b_scr[si]], writes=[b_r])
        return r, b_r

    pair_ctr = [0]

    def next_pair():
        p = pair[pair_ctr[0] % 3]
        pair_ctr[0] += 1
        return p

    def rmsnorm_to(x_t, b_x, gain_tile, b_g, out_t, b_out):
        op("act", lambda e: e.activation(out=sqj[:], in_=x_t[:], func=AF.Square, accum_out=stat[:, 0:1]),
           reads=[b_x], writes=[b_sqj, b_stat])
        op("dve", lambda e: e.tensor_scalar(out=stat[:, 1:2], in0=stat[:, 0:1], scalar1=1.0 / D, scalar2=EPS,
                                            op0=ALU.mult, op1=ALU.add), reads=[b_stat], writes=[b_stat])
        op("pool", lambda e: e.tensor_tensor(out=stat[:, 2:3], in0=stat[:, 1:2], in1=neghalf[:], op=ALU.pow),
           reads=[b_stat, b_neghalf], writes=[b_stat])
        op("dve", lambda e: e.scalar_tensor_tensor(out=out_t[:], in0=x_t[:], scalar=stat[:, 2:3], in1=gain_tile[:],
                                                   op0=ALU.mult, op1=ALU.mult),
           reads=[b_x, b_stat, b_g], writes=[b_out])

    def norm_transpose(x_t, b_x, col):
        rmsnorm_to(x_t, b_x, gain_t, b_gain, hn, b_hn)
        for k in range(8):
            op("pe", lambda e, k=k: e.transpose(out=PB0[:, k, :], in_=hn[:, k * 128:(k + 1) * 128], identity=identb[:]),
               reads=[b_hn, b_identb], writes=[b_PB0])
        op("act", lambda e: e.activation(out=hnT[:, :, col:col + 128], in_=PB0[:], func=AF.Copy),
           reads=[b_PB0], writes=[b_hnT])

    idx_scale = (8 ** -0.5) * (64 ** -0.5)

    def slot(j):
        nk = 256 * (j + 1)
        pos0 = (2 * j) % 8
        for i in range(2):
            xt, b_xt = xin[i]
            t = 2 * j + i
            dma("sp", lambda e, xt=xt, t=t: e.dma_start(out=xt[:], in_=xf[t * 128:(t + 1) * 128, :]), writes=[b_xt])
            norm_transpose(xt, b_xt, i * 128)
        dma("sp", lambda e: e.dma_start(out=xown[:], in_=xo[j * 128:(j + 1) * 128, :]), writes=[b_xown])
        norm_transpose(xown, b_xown, 256)

        rA, b_rA = stream(0)
        for blk in range(4):
            pr, b_pr = next_pair()
            for k in range(8):
                op("pe", lambda e, k=k, blk=blk, pr=pr, rA=rA: e.matmul(
                    pr[:, 0:256], lhsT=rA[:, k * 512 + blk * 128:k * 512 + (blk + 1) * 128], rhs=hnT[:, k, 0:256],
                    start=(k == 0), stop=(k == 7)), reads=[b_rA, b_hnT], writes=[b_pr])
            op("act", lambda e, blk=blk, pr=pr: e.activation(
                out=kaT[:, blk, pos0 * 128:pos0 * 128 + 256], in_=pr[:, 0:256], func=AF.Copy),
                reads=[b_pr], writes=[b_kaT])
        rB, b_rB = stream(1)
        pr, b_pr = next_pair()
        for k in range(8):
            op("pe", lambda e, k=k, pr=pr, rB=rB: e.matmul(
                pr[:, 0:256], lhsT=rB[:, k * 512:k * 512 + 128], rhs=hnT[:, k, 0:256],
                start=(k == 0), stop=(k == 7)), reads=[b_rB, b_hnT], writes=[b_pr])
        op("act", lambda e, pr=pr: e.activation(
            out=kkT[:, 2 * j * 128:2 * j * 128 + 256], in_=pr[:, 0:256], func=AF.Copy),
            reads=[b_pr], writes=[b_kkT])
        for i in range(2):
            pr, b_pr = next_pair()
            for k in range(8):
                op("pe", lambda e, k=k, i=i, pr=pr, rB=rB: e.matmul(
                    pr[:, 0:64], lhsT=hnT[:, k, i * 128:(i + 1) * 128], rhs=rB[:, k * 512 + 128:k * 512 + 192],
                    start=(k == 0), stop=(k == 7)), reads=[b_rB, b_hnT], writes=[b_pr])
            op("dve", lambda e, i=i, pr=pr: e.tensor_copy(out=vB[:, 2 * j + i, 0:64], in_=pr[:, 0:64]),
               reads=[b_pr], writes=[b_vB])
        rC, b_rC = stream(2)
        for i in range(2):
            pr, b_pr = next_pair()
            for k in range(8):
                op("pe", lambda e, k=k, i=i, pr=pr, rC=rC: e.matmul(
                    pr[:, 0:512], lhsT=hnT[:, k, i * 128:(i + 1) * 128], rhs=rC[:, k * 512:(k + 1) * 512],
                    start=(k == 0), stop=(k == 7)), reads=[b_rC, b_hnT], writes=[b_pr])
            op("dve", lambda e, i=i, pr=pr: e.tensor_copy(
                out=vA[:, pos0 + i, :, 0:64], in_=pr[:, 0:512].rearrange("p (h d) -> p h d", h=8)),
                reads=[b_pr], writes=[b_vA])

        for grp in range(3):
            rQ, b_rQ = stream(3 + grp)
            pr, b_pr = next_pair()
            for b4 in range(4):
                blk = grp * 4 + b4
                for k in range(8):
                    op("pe", lambda e, k=k, b4=b4, pr=pr, rQ=rQ: e.matmul(
                        pr[:, b4 * 128:(b4 + 1) * 128], lhsT=rQ[:, k * 512 + b4 * 128:k * 512 + (b4 + 1) * 128],
                        rhs=hnT[:, k, 256:384], start=(k == 0), stop=(k == 7)),
                        reads=[b_rQ, b_hnT], writes=[b_pr])
            src = lambda lo, hi, pr=pr: pr[lo:hi, 0:512].rearrange("p (b q) -> p b q", b=4)
            if grp == 0:
                op("act", lambda e, src=src: e.activation(out=qaZ[0][0][0:64, :, :], in_=src(0, 64), func=AF.Copy,
                                                          scale=0.125), reads=[b_pr], writes=[qaZ[0][1]])
                op("act", lambda e, src=src: e.activation(out=qaZ[1][0][64:128, :, :], in_=src(64, 128), func=AF.Copy,
                                                          scale=0.125), reads=[b_pr], writes=[qaZ[1][1]])
            else:
                h0 = (grp - 1) * 4
                op("act", lambda e, src=src, h0=h0: e.activation(out=qbZ[0:64, h0:h0 + 4, :], in_=src(0, 64),
                                                                 func=AF.Copy, scale=0.125),
                   reads=[b_pr], writes=[b_qbZ])
                op("act", lambda e, pr=pr, h0=h0: e.activation(
                    out=qiZ[64:128, :, h0:h0 + 4, :],
                    in_=pr[64:128, 0:512].rearrange("p (h g q) -> p g h q", h=4, g=8), func=AF.Copy),
                    reads=[b_pr], writes=[b_qiZ])
        pr, b_pr = next_pair()
        for k in range(8):
            op("pe", lambda e, k=k, pr=pr: e.matmul(pr[:, 0:8], lhsT=hnT[:, k, 256:384], rhs=Wwi[:, k, :],
                                                    start=(k == 0), stop=(k == 7)),
               reads=[b_Wwi, b_hnT], writes=[b_pr])
        op("act", lambda e, pr=pr: e.activation(out=wsc[:], in_=pr[:, 0:8], func=AF.Copy, scale=idx_scale),
           reads=[b_pr], writes=[b_wsc])
        op("dve", lambda e: e.tensor_tensor(
            out=Tb[:], in0=wsc[:].unsqueeze(1).unsqueeze(3).to_broadcast([128, 8, 8, 16]),
            in1=dsel[:].unsqueeze(2).to_broadcast([128, 8, 8, 16]), op=ALU.mult),
            reads=[b_wsc, b_dsel], writes=[b_Tb])
        for g in range(8):
            op("pe", lambda e, g=g: e.transpose(out=PB0[:, g, :], in_=Tb[:, g, :, :].rearrange("p h q -> p (h q)"),
                                                identity=identb[:]), reads=[b_Tb, b_identb], writes=[b_PB0])
        op("act", lambda e: e.activation(out=Wblk[:], in_=PB0[:], func=AF.Copy), reads=[b_PB0], writes=[b_Wblk])

        for ci in range(6):
            r, b_r = stream(6 + ci)
            pr, b_pr = next_pair()
            for k in range(8):
                op("pe", lambda e, k=k, r=r, pr=pr: e.matmul(
                    pr[:, 0:512], lhsT=hnT[:, k, 256:384], rhs=r[:, k * 512:(k + 1) * 512],
                    start=(k == 0), stop=(k == 7)), reads=[b_r, b_hnT], writes=[b_pr])
            op("act", lambda e, pr=pr: e.activation(out=tmpa[:], in_=pr[:, 0:512], func=AF.Tanh, scale=0.5),
               reads=[b_pr], writes=[b_tmpa])
            if ci < 2:
                dst, b_dst = (sza, b_sza) if ci == 0 else (szb, b_szb)
                op("dve", lambda e: e.tensor_scalar(out=tmpb[:], in0=tmpa[:], scalar1=0.5, scalar2=0.5,
                                                    op0=ALU.mult, op1=ALU.add), reads=[b_tmpa], writes=[b_tmpb])
                op("dve", lambda e, dst=dst, pr=pr: e.tensor_tensor(out=dst[:], in0=tmpb[:], in1=pr[:, 0:512],
                                                                    op=ALU.mult),
                   reads=[b_tmpb, b_pr], writes=[b_dst])
            else:
                dst, b_dst = (sga, b_sga) if ci < 4 else (sgb, b_sgb)
                c0 = (ci % 2) * 512
                op("dve", lambda e, dst=dst, c0=c0: e.tensor_scalar(
                    out=dst[:, c0:c0 + 512], in0=tmpa[:], scalar1=0.5, scalar2=0.5, op0=ALU.mult, op1=ALU.add),
                    reads=[b_tmpa], writes=[b_dst])

        rvalid = [r for r in range(6) if 2 * j - 4 + r >= 0]
        nr = len(rvalid)
        pv, b_pv = pair[2]

        def a_front(h):
            p, e2 = h // 2, h % 2
            sp_, b_sp = pair[h % 2]
            for idx, r in enumerate(rvalid):
                kp = (2 * j - 4 + r) % 8
                op("pe", lambda e, idx=idx, p=p, kp=kp, e2=e2, sp_=sp_: e.matmul(
                    sp_[:, idx * 128:(idx + 1) * 128], lhsT=kaT[:, p, kp * 128:(kp + 1) * 128],
                    rhs=qaZ[e2][0][:, p, :], start=True, stop=False),
                    reads=[b_kaT, qaZ[e2][1]], writes=[b_sp])
                op("pe", lambda e, idx=idx, r=r, h=h, sp_=sp_: e.matmul(
                    sp_[:, idx * 128:(idx + 1) * 128], lhsT=identb[:], rhs=biasAm[:, r, h, :],
                    start=False, stop=True), reads=[b_identb, b_biasAm], writes=[b_sp])

        def a_back(h):
            sp_, b_sp = pair[h % 2]
            pa, b_pa = PAs[h % 2]
            op("act", lambda e, sp_=sp_, pa=pa: e.activation(
                out=pa[:, 0:nr, :].rearrange("p r q -> p (r q)"), in_=sp_[:, 0:nr * 128], func=AF.Exp),
                reads=[b_sp], writes=[b_pa])
            for idx, r in enumerate(rvalid):
                kp = (2 * j - 4 + r) % 8
                op("pe", lambda e, idx=idx, kp=kp, h=h, pa=pa: e.matmul(
                    pv[:, h * 128:h * 128 + 65], lhsT=pa[:, idx, :], rhs=vA[:, kp, h, 0:65],
                    start=(idx == 0), stop=(idx == nr - 1)), reads=[b_pa, b_vA], writes=[b_pv])

        a_front(0)
        for h in range(8):
            if h + 1 < 8:
                a_front(h + 1)
            a_back(h)

        def finish_branch(pv, b_pv, sz, b_sz, yg, b_yg):
            pv3 = pv[:, :].rearrange("p (h c) -> p h c", h=8)
            op("dve", lambda e: e.reciprocal(out=rden[:], in_=pv3[:, :, 64]), reads=[b_pv], writes=[b_rden])
            op("dve", lambda e: e.tensor_tensor(out=yun[:], in0=pv3[:, :, 0:64],
                                                in1=rden[:].unsqueeze(2).to_broadcast([128, 8, 64]), op=ALU.mult),
               reads=[b_pv, b_rden], writes=[b_yun])
            op("dve", lambda e: e.tensor_tensor(out=yg[:], in0=yun[:].rearrange("p h d -> p (h d)"), in1=sz[:],
                                                op=ALU.mult), reads=[b_yun, b_sz], writes=[b_yg])

        finish_branch(pv, b_pv, sza, b_sza, yag, b_yag)

        nblk = (nk + 511) // 512
        items = [(blk, g) for blk in range(nblk) for g in range(8)]
        sacc, b_sacc = pair[2]

        def i_front(n):
            blk, g = items[n]
            w = min(512, nk - 512 * blk)
            k0 = blk * 512
            lg, b_lg = pair[n % 2]
            op("pe", lambda e, g=g, w=w, k0=k0, lg=lg: e.matmul(
                lg[:, 0:w], lhsT=qiZ[:, g, :, :].rearrange("p h q -> p (h q)"), rhs=kkT[:, k0:k0 + w],
                start=True, stop=True), reads=[b_qiZ, b_kkT], writes=[b_lg])

        def i_back(n):
            blk, g = items[n]
            w = min(512, nk - 512 * blk)
            k0 = blk * 512
            lg, b_lg = pair[n % 2]
            R_, b_R = Rr[n % 3]
            op("act", lambda e, w=w, lg=lg, R_=R_: e.activation(out=R_[:, 0:w], in_=lg[:, 0:w], func=AF.Relu),
               reads=[b_lg], writes=[b_R])
            op("pe", lambda e, g=g, w=w, R_=R_: e.matmul(
                sacc[:, 0:w], lhsT=Wblk[:, g, :], rhs=R_[:, 0:w], start=(g == 0), stop=(g == 7)),
                reads=[b_Wblk, b_R], writes=[b_sacc])
            if g == 7:
                last = (blk == nblk - 1)
                wcopy = w - 256 if last else w
                if wcopy > 0:
                    op("act", lambda e, k0=k0, wcopy=wcopy: e.activation(
                        out=scores[:, k0:k0 + wcopy], in_=sacc[:, 0:wcopy], func=AF.Copy),
                        reads=[b_sacc], writes=[b_scores])
                if last:
                    op("dve", lambda e, w=w: e.tensor_tensor(
                        out=scores[:, nk - 256:nk], in0=sacc[:, w - 256:w], in1=admB[:], op=ALU.add),
                        reads=[b_sacc, b_admB], writes=[b_scores])

        i_front(0)
        for n in range(len(items)):
            if n + 1 < len(items):
                i_front(n + 1)
            i_back(n)

        op("dve", lambda e: e.memset(cand[0][0][:], 0.0), writes=[cand[0][1]])
        step = BIS_M
        for it in range(NBIS):
            ca, b_ca = cand[it % 2]
            cb, b_cb = cand[(it + 1) % 2]
            op("dve", lambda e, ca=ca: e.tensor_scalar(
                out=maskf[:, 0:nk], in0=scores[:, 0:nk], scalar1=ca[:, 0:1], scalar2=None, op0=ALU.is_ge,
                op1=ALU.add, accum_out=cnt[:, 0:1]), reads=[b_scores, b_ca], writes=[b_maskf, b_cnt])
            op("dve", lambda e: e.tensor_scalar(out=pmh[:], in0=cnt[:], scalar1=float(nsel), scalar2=0.5,
                                                op0=ALU.is_ge, op1=ALU.subtract), reads=[b_cnt], writes=[b_pmh])
            op("dve", lambda e, ca=ca, cb=cb, step=step: e.scalar_tensor_tensor(
                out=cb[:], in0=pmh[:], scalar=float(step), in1=ca[:], op0=ALU.mult, op1=ALU.add),
                reads=[b_pmh, b_ca], writes=[b_cb])
            step = step / 2.0
        cf, b_cf = cand[NBIS % 2]
        op("dve", lambda e, cf=cf, step=step: e.tensor_scalar(out=thr[:], in0=cf[:], scalar1=-float(step),
                                                              scalar2=None, op0=ALU.add),
           reads=[b_cf], writes=[b_thr])
        op("dve", lambda e: e.tensor_scalar(out=maskf[:, 0:nk], in0=scores[:, 0:nk], scalar1=thr[:, 0:1],
                                            scalar2=None, op0=ALU.is_ge),
           reads=[b_scores, b_thr], writes=[b_maskf])

        pvb, b_pvb = pair[2]
        nkt = 2 * j + 2

        def b_front(kt):
            s4 = kt % 4
            sp_, b_sp = pair[kt % 2]
            near = kt >= 2 * j - 1
            for half in range(2):
                op("pe", lambda e, kt=kt, half=half, near=near, sp_=sp_: e.matmul(
                    sp_[:, half * 512:(half + 1) * 512], lhsT=kkT[:, kt * 128:(kt + 1) * 128],
                    rhs=qbZ[:, 4 * half:4 * half + 4, :].rearrange("p h q -> p (h q)"), start=True, stop=(not near)),
                    reads=[b_kkT, b_qbZ], writes=[b_sp])
                if near:
                    r = kt - (2 * j - 1)
                    op("pe", lambda e, half=half, r=r, sp_=sp_: e.matmul(
                        sp_[:, half * 512:(half + 1) * 512], lhsT=identb[:],
                        rhs=biasTn[:, r, 4 * half:4 * half + 4, :].rearrange("p h q -> p (h q)"), start=False, stop=True),
                        reads=[b_identb, b_biasTn], writes=[b_sp])
            op("pe", lambda e, kt=kt, s4=s4: e.transpose(out=PB1[:, s4, :], in_=maskf[:, kt * 128:(kt + 1) * 128],
                                                         identity=identb[:]),
               reads=[b_maskf, b_identb], writes=[b_PB1])

        def b_back(kt):
            s4 = kt % 4
            sp_, b_sp = pair[kt % 2]
            pt, b_pt = PT[kt % 2]
            ptm, b_ptm = PTm[kt % 2]
            op("act", lambda e, sp_=sp_, pt=pt: e.activation(out=pt[:].rearrange("p h q -> p (h q)"),
                                                             in_=sp_[:, :], func=AF.Exp),
               reads=[b_sp], writes=[b_pt])
            op("dve", lambda e, s4=s4, pt=pt, ptm=ptm: e.tensor_tensor(
                out=ptm[:], in0=pt[:], in1=PB1[:, s4, :].unsqueeze(1).to_broadcast([128, 8, 128]), op=ALU.mult),
                reads=[b_pt, b_PB1], writes=[b_ptm])
            for half in range(2):
                op("pe", lambda e, kt=kt, half=half, ptm=ptm: e.matmul(
                    pvb[0:65, half * 512:(half + 1) * 512], lhsT=vB[:, kt, 0:65],
                    rhs=ptm[:, 4 * half:4 * half + 4, :].rearrange("p h q -> p (h q)"),
                    start=(kt == 0), stop=(kt == nkt - 1)), reads=[b_ptm, b_vB], writes=[b_pvb])

        b_front(0)
        for kt in range(nkt):
            if kt + 1 < nkt:
                b_front(kt + 1)
            b_back(kt)
        op("act", lambda e: e.activation(out=hres[0:65, :], in_=pvb[0:65, :], func=AF.Copy),
           reads=[b_pvb], writes=[b_hres])
        pvt, b_pvt = pair[0]
        for h in range(8):
            op("pe", lambda e, h=h: e.transpose(out=pvt[:, h * 128:h * 128 + 65], in_=hres[0:65, h * 128:(h + 1) * 128],
                                                identity=identf[0:65, 0:65]),
               reads=[b_hres, b_identf], writes=[b_pvt])
        finish_branch(pvt, b_pvt, szb, b_szb, ybg, b_ybg)
        pair_ctr[0] = 0

        def out_proj_branch(yg, b_yg, si, sg, b_sg, first):
            for c4 in range(4):
                op("pe", lambda e, c4=c4: e.transpose(out=PB0[:, c4, :], in_=yg[:, c4 * 128:(c4 + 1) * 128],
                                                      identity=identb[:]), reads=[b_yg, b_identb], writes=[b_PB0])
            op("act", lambda e: e.activation(out=yT[:], in_=PB0[:, 0:4, :], func=AF.Copy), reads=[b_PB0], writes=[b_yT])
            r, b_r = stream(si)
            pr, b_pr = next_pair()
            for half in range(2):
                for c4 in range(4):
                    op("pe", lambda e, c4=c4, half=half, r=r, pr=pr: e.matmul(
                        pr[:, half * 512:(half + 1) * 512], lhsT=yT[:, c4, :],
                        rhs=r[:, c4 * 1024 + half * 512:c4 * 1024 + (half + 1) * 512],
                        start=(c4 == 0), stop=(c4 == 3)), reads=[b_yT, b_r], writes=[b_pr])
            if first:
                op("dve", lambda e, pr=pr: e.tensor_tensor(out=merged[:], in0=pr[:, :], in1=sg[:], op=ALU.mult),
                   reads=[b_pr, b_sg], writes=[b_merged])
            else:
                op("dve", lambda e, pr=pr: e.tensor_tensor(out=hres[:], in0=pr[:, :], in1=sg[:], op=ALU.mult),
                   reads=[b_pr, b_sg], writes=[b_hres])
                op("dve", lambda e: e.tensor_tensor(out=mergedb[:], in0=merged[:], in1=hres[:], op=ALU.add),
                   reads=[b_merged, b_hres], writes=[b_mergedb])

        out_proj_branch(yag, b_yag, 12, sga, b_sga, True)
        out_proj_branch(ybg, b_ybg, 13, sgb, b_sgb, False)
        for k in range(8):
            op("pe", lambda e, k=k: e.transpose(out=PB0[:, k, :], in_=mergedb[:, k * 128:(k + 1) * 128],
                                                identity=identb[:]), reads=[b_mergedb, b_identb], writes=[b_PB0])
        op("act", lambda e: e.activation(out=mT[:], in_=PB0[:], func=AF.Copy), reads=[b_PB0], writes=[b_mT])
        pr, b_pr = next_pair()
        for wi_ in range(2):
            r, b_r = stream(14 + wi_)
            for half in range(2):
                for c4 in range(4):
                    k = wi_ * 4 + c4
                    op("pe", lambda e, c4=c4, k=k, half=half, r=r, pr=pr: e.matmul(
                        pr[:, half * 512:(half + 1) * 512], lhsT=mT[:, k, :],
                        rhs=r[:, c4 * 1024 + half * 512:c4 * 1024 + (half + 1) * 512],
                        start=(k == 0), stop=(k == 7), skip_group_check=True), reads=[b_mT, b_r], writes=[b_pr])
        op("dve", lambda e, pr=pr: e.tensor_tensor(out=hres[:], in0=pr[:, :], in1=xown[:], op=ALU.add),
           reads=[b_pr, b_xown], writes=[b_hres])
        rmsnorm_to(hres, b_hres, fgain_t, b_fgain, yout, b_yout)
        b_o = Buf("o%d" % j)
        dma("sp", lambda e: e.dma_start(out=outd[j * 128:(j + 1) * 128, :], in_=yout[:]), reads=[b_yout], writes=[b_o])
        obufs.append(b_o)
    for j_ in range(NS):
        slot(j_)
    sch.final_wait("sp", obufs)
    sch.emit(nc)
    st.close()
    return nc


def _t5_bucket(rel):
    half, max_exact = 16, 8
    ret = np.where(rel > 0, half, 0)
    n = np.abs(rel)
    nf = np.maximum(n, 1).astype(np.float32)
    large = max_exact + (np.log(nf / np.float32(max_exact)) / np.float32(math.log(128 / max_exact))
                         * np.float32(half - max_exact)).astype(np.int32)
    large = np.minimum(large, half - 1)
    return ret + np.where(n < max_exact, n, large)


def prep_shared(norm_gain, w_in, w_a_out, w_b_out, w_out, final_gain):
    W = np.asarray(w_in[0])
    offs = np.cumsum([0, 512, 512, 512, 512, 512, 64, 64, 512, 512, 64, 8, 1024, 1024])
    names = ["qa", "ka", "va", "za", "qb", "kb", "vb", "zb", "qi", "ki", "wi", "ga", "gb"]
    col = {n: np.arange(offs[i], offs[i + 1]) for i, n in enumerate(names)}
    qbqi = np.concatenate([np.concatenate([col["qb"][h * 64:(h + 1) * 64], col["qi"][h * 64:(h + 1) * 64]])
                           for h in range(8)])
    sh = {
        "wkfm": np.ascontiguousarray(W[:, np.concatenate([col["ka"], col["kb"], col["ki"]])]),
        "wktm": np.ascontiguousarray(W[:, np.concatenate([col["va"], col["vb"]])]),
        "wqfm": np.ascontiguousarray(W[:, np.concatenate([col["qa"], qbqi])]),
        "wqtm": np.ascontiguousarray(W[:, np.concatenate([col["za"], col["zb"], col["ga"], col["gb"]])]),
        "wwi": np.ascontiguousarray(W[:, col["wi"]]),
        "wao": np.ascontiguousarray(w_a_out[0]),
        "wbo": np.ascontiguousarray(w_b_out[0]),
        "wo": np.ascontiguousarray(w_out[0]),
        "gain": np.ascontiguousarray(np.broadcast_to(np.asarray(norm_gain[0])[None, :], (128, D))),
        "fgain": np.ascontiguousarray(np.broadcast_to(np.asarray(final_gain)[None, :], (128, D))),
        "ident": np.eye(128, dtype=np.float32),
    }
    ds = np.zeros((128, 8, 16), np.float32)
    for q in range(128):
        ds[q, q // 16, q % 16] = 1.0
    sh["dsel"] = ds.reshape(128, 128)
    return sh


def prep_core_consts(c, a_rel_bias, t5_bias):
    arb = np.asarray(a_rel_bias[0])
    tb = np.asarray(t5_bias)
    so = np.arange(128)[:, None]
    to = np.arange(128)[None, :]
    biasA = np.zeros((128, 6, 8, 128), np.float32)
    maskA = np.zeros((128, 6, 128), np.float32)
    for r in range(6):
        rel = (c + 4 - r) * 128 + to - so
        idx = np.clip(rel, -256, 256) + 256
        biasA[:, r, :, :] = np.transpose(arb[:, idx], (1, 0, 2))
        diff = 2 * c + 8 - 2 * r + (to >= 64).astype(int) - (so >= 64).astype(int)
        maskA[:, r, :] = np.where((diff >= 0) & (diff <= 8), 0.0, -30000.0)
    biasT = np.zeros((128, 3, 8, 128), np.float32)
    for r in range(3):
        rel = (r - 1 - c) * 128 + so - to
        bk = _t5_bucket(rel.astype(np.int32))
        biasT[:, r, :, :] = np.transpose(tb[bk], (0, 2, 1))
    cT = np.ascontiguousarray(np.broadcast_to(tb[15][None, :], (128, 8)))
    adm = np.zeros((128, 256), np.float32)
    qh = (np.arange(128) >= 64).astype(int)[:, None]
    kc4 = (np.arange(256) // 64)[None, :]
    adm[:] = np.where(kc4 <= 2 * c + qh, 0.0, -1e30)
    return {"biasA": biasA.reshape(128, -1), "maskA": maskA.reshape(128, -1),
            "biasT": biasT.reshape(128, -1), "cT": cT, "admB": adm}


def make_in_maps(x, norm_gain, w_in, a_rel_bias, t5_bias, w_a_out, w_b_out, w_out, final_gain):
    x = np.asarray(x)
    B, S, _ = x.shape
    sh = prep_shared(norm_gain, w_in, w_a_out, w_b_out, w_out, final_gain)
    cc = [prep_core_consts(c, a_rel_bias, t5_bias) for c in range(2)]
    maps = []
    for b in range(B):
        xb = x[b].reshape(S // 128, 128, D)
        for c in range(2):
            m = dict(sh)
            m.update(cc[c])
            m["xf"] = np.ascontiguousarray(x[b])
            m["xo"] = np.ascontiguousarray(xb[c::2].reshape(S // 2, D))
            maps.append(m)
    return maps


def assemble(results, B, S):
    out = np.zeros((B, S // 128, 128, D), np.float32)
    i = 0
    for b in range(B):
        for c in range(2):
            out[b, c::2] = np.asarray(results[i]["out"]).reshape(S // 256, 128, D)
            i += 1
    return out.reshape(B, S, D)


def kernel(x, norm_gain, w_in, a_rel_bias, t5_bias, w_a_out, w_b_out, w_out, final_gain):
    x = np.asarray(x)
    B, S, _ = x.shape
    nc = build(S, min(256, S // 4))
    maps = make_in_maps(x, norm_gain, w_in, a_rel_bias, t5_bias, w_a_out, w_b_out, w_out, final_gain)
    res = run_bass_kernel_spmd(nc, maps, core_ids=list(range(len(maps))))
    return assemble(res.results, B, S)
```

```python
import math
from contextlib import ExitStack

import numpy as np
import concourse.bass as bass
import concourse.mybir as mybir
from concourse.bass_utils import run_bass_kernel_spmd

F32 = mybir.dt.float32
BF16 = mybir.dt.bfloat16
ALU = mybir.AluOpType
AF = mybir.ActivationFunctionType

D = 1024
EPS = 1e-6
NBIS = 18
BIS_M = 16.0


class Buf:
    __slots__ = ("name", "last_write", "readers")

    def __init__(self, name):
        self.name = name
        self.last_write = None
        self.readers = {}


class _Eng:
    def __init__(self, name):
        self.name = name
        self.count = 0
        self.ops = []
        self.waited = {}


class Sched:
    N_DMA_SLOTS = 12

    def __init__(self):
        self.engs = {k: _Eng(k) for k in ("pe", "act", "dve", "pool", "sp")}
        self.dma_slot_count = {}
        self.dma_next = {k: 0 for k in self.engs}
        self.semkeys = set()

    def _deps(self, E, reads, writes):
        deps = {}

        def add(tok):
            if tok is None:
                return
            k, v = tok
            if deps.get(k, 0) < v:
                deps[k] = v

        for b in reads:
            add(b.last_write)
        for b in writes:
            add(b.last_write)
            for k, v in b.readers.items():
                add((k, v))
        out = []
        for k, v in deps.items():
            if k == ("eng", E.name) and E.name in ("pe", "sp"):
                continue
            if E.waited.get(k, 0) >= v:
                continue
            E.waited[k] = v
            out.append((k, v))
        return out

    def _commit(self, tok, reads, writes):
        k, v = tok
        for b in writes:
            b.last_write = tok
            b.readers = {}
        for b in reads:
            if b.readers.get(k, 0) < v:
                b.readers[k] = v

    def op(self, eng, fn, reads=(), writes=()):
        E = self.engs[eng]
        waits = self._deps(E, reads, writes)
        E.count += 1
        key = ("eng", E.name)
        self.semkeys.add(key)

        def emit(e, sems, waits=waits, fn=fn, key=key):
            for (k, v) in waits:
                e.wait_ge(sems[k], v)
            fn(e).then_inc(sems[key], 1)

        E.ops.append(emit)
        self._commit((key, E.count), reads, writes)

    def dma(self, queue, fn, reads=(), writes=()):
        E = self.engs[queue]
        slot = self.dma_next[queue] % self.N_DMA_SLOTS
        self.dma_next[queue] += 1
        key = ("dma", queue, slot)
        self.semkeys.add(key)
        n = self.dma_slot_count.get(key, 0)
        waits = self._deps(E, reads, writes)
        if n > 0 and E.waited.get(key, 0) < 16 * n:
            E.waited[key] = 16 * n
            waits.append((key, 16 * n))
        self.dma_slot_count[key] = n + 1

        def emit(e, sems, waits=waits, fn=fn, key=key):
            for (k, v) in waits:
                e.wait_ge(sems[k], v)
            fn(e).then_inc(sems[key], 16)

        E.ops.append(emit)
        self._commit((key, 16 * (n + 1)), reads, writes)

    def final_wait(self, eng, bufs):
        E = self.engs[eng]
        waits = self._deps(E, bufs, ())

        def emit(e, sems, waits=waits):
            for (k, v) in waits:
                e.wait_ge(sems[k], v)

        E.ops.append(emit)

    def emit(self, nc):
        with ExitStack() as st:
            sems = {}
            for i, k in enumerate(sorted(self.semkeys, key=str)):
                sems[k] = st.enter_context(nc.semaphore("s%d" % i))
            block = st.enter_context(nc.Block())
            engs = self.engs

            @block.tensor
            def _(e):
                for f in engs["pe"].ops:
                    f(e, sems)

            @block.scalar
            def _(e):
                for f in engs["act"].ops:
                    f(e, sems)

            @block.vector
            def _(e):
                for f in engs["dve"].ops:
                    f(e, sems)

            @block.gpsimd
            def _(e):
                for f in engs["pool"].ops:
                    f(e, sems)

            @block.sync
            def _(e):
                for f in engs["sp"].ops:
                    f(e, sems)


def build(S, nsel):
    NT = S // 128
    NS = S // 256
    nc = bass.Bass("TRN2", target_bir_lowering=False)

    def dram(name, shape, dt=F32, kind="ExternalInput"):
        return nc.dram_tensor(name, shape, dt, kind=kind).ap()

    xf = dram("xf", [S, D])
    xo = dram("xo", [S // 2, D])
    wkfm = dram("wkfm", [D, 640])
    wktm = dram("wktm", [D, 576])
    wqfm = dram("wqfm", [D, 1536])
    wqtm = dram("wqtm", [D, 3072])
    wwi = dram("wwi", [D, 8])
    wao = dram("wao", [512, D])
    wbo = dram("wbo", [512, D])
    wo = dram("wo", [D, D])
    gain = dram("gain", [128, D])
    fgain = dram("fgain", [128, D])
    identd = dram("ident", [128, 128])
    biasAd = dram("biasA", [128, 6 * 8 * 128])
    maskAd = dram("maskA", [128, 6 * 128])
    biasTd = dram("biasT", [128, 3 * 8 * 128])
    cTd = dram("cT", [128, 8])
    admBd = dram("admB", [128, 256])
    dseld = dram("dsel", [128, 128])
    outd = dram("out", [S // 2, D], kind="ExternalOutput")
    scr = [dram("scr%d" % i, [128, 4096], BF16, kind="Internal") for i in range(16)]
    b_scr = [Buf("scr%d" % i) for i in range(16)]

    sch = Sched()
    st = ExitStack()

    def sb(name, shape, dt):
        return st.enter_context(nc.sbuf_tensor(name, shape, dt)), Buf(name)

    def ps(name, shape, dt):
        return st.enter_context(nc.psum_tensor(name, shape, dt)), Buf(name)

    identb, b_identb = sb("identb", [128, 128], BF16)
    identf, b_identf = sb("identf", [128, 128], F32)
    gain_t, b_gain = sb("gain_t", [128, D], F32)
    fgain_t, b_fgain = sb("fgain_t", [128, D], F32)
    Wwi, b_Wwi = sb("Wwi", [128, 8, 8], BF16)
    ring = [sb("ring%d" % i, [128, 4096], BF16) for i in range(2)]
    scores, b_scores = sb("scores", [128, max(S, 8192)], F32)
    maskf, b_maskf = sb("maskf", [128, S], BF16)
    kkT, b_kkT = sb("kkT", [128, S], BF16)
    vB, b_vB = sb("vB", [128, NT, 66], BF16)
    kaT, b_kaT = sb("kaT", [128, 4, 1024], BF16)
    vA, b_vA = sb("vA", [128, 8, 8, 66], BF16)
    biasAm, b_biasAm = sb("biasAm", [128, 6, 8, 128], BF16)
    biasTn, b_biasTn = sb("biasTn", [128, 3, 8, 128], BF16)
    admB, b_admB = sb("admB_t", [128, 256], F32)
    dsel, b_dsel = sb("dsel_t", [128, 8, 16], F32)
    neghalf, b_neghalf = sb("neghalf", [128, 1], F32)
    xin = [sb("xin%d" % i, [128, D], F32) for i in range(1)] * 2
    xown, b_xown = sb("xown", [128, D], F32)
    sqj, b_sqj = sb("sqj", [128, D], BF16)
    hn, b_hn = sb("hn", [128, D], BF16)
    hnT, b_hnT = sb("hnT", [128, 8, 384], BF16)
    stat, b_stat = sb("stat", [128, 8], F32)
    qaZ = [sb("qaZ%d" % i, [128, 4, 128], BF16) for i in range(2)]
    qbZ, b_qbZ = sb("qbZ", [128, 8, 128], BF16)
    qiZ, b_qiZ = sb("qiZ", [128, 8, 8, 16], BF16)
    tmpa, b_tmpa = sb("tmpa", [128, 512], F32)
    tmpb, b_tmpb = tmpa, b_tmpa
    sza, b_sza = sb("sza", [128, 512], F32)
    szb, b_szb = sb("szb", [128, 512], F32)
    sga, b_sga = sb("sga", [128, D], F32)
    sgb, b_sgb = sb("sgb", [128, D], F32)
    wsc, b_wsc = sb("wsc", [128, 8], F32)
    Tb, b_Tb = sb("Tb", [128, 8, 8, 16], BF16)
    Wblk, b_Wblk = sb("Wblk", [128, 8, 128], BF16)
    Rr = [sb("R%d" % i, [128, 512], BF16) for i in range(3)]
    PAs = [sb("PA%d" % i, [128, 6, 128], BF16) for i in range(2)]
    PT = [sb("PT%d" % i, [128, 8, 128], BF16) for i in range(2)]
    PTm = [sb("PTm%d" % i, [128, 8, 128], BF16) for i in range(2)]
    cand = [sb("cand%d" % i, [128, 1], F32) for i in range(2)]
    cnt, b_cnt = sb("cnt", [128, 1], F32)
    cntA, b_cntA = sb("cntA", [128, 1], F32)
    ncand = [sb("ncand%d" % i, [128, 1], F32) for i in range(2)]
    b_maskf_hi = Buf("maskf_hi")
    pmh, b_pmh = sb("pmh", [128, 1], F32)
    thr, b_thr = sb("thr", [128, 1], F32)
    rden, b_rden = sb("rden", [128, 8], F32)
    yun, b_yun = sb("yun", [128, 8, 64], F32)
    yag, b_yag = sb("yag", [128, 512], BF16)
    ybg, b_ybg = sb("ybg", [128, 512], BF16)
    yT, b_yT = sb("yT", [128, 4, 128], BF16)
    merged, b_merged = sb("merged", [128, D], F32)
    mergedb, b_mergedb = sb("mergedb", [128, D], BF16)
    mT, b_mT = sb("mT", [128, 8, 128], BF16)
    hres, b_hres = sb("hres", [128, D], F32)
    yout, b_yout = merged, b_merged
    pair = [ps("pair%d" % i, [128, 1024], F32) for i in range(3)]
    PB0, b_PB0 = ps("PB0", [128, 8, 128], BF16)
    PB1, b_PB1 = ps("PB1", [128, 8, 128], BF16)
    b_PB1s = [Buf("PB1s%d" % i) for i in range(4)]

    op = sch.op
    dma = sch.dma
    obufs = []

    dma("sp", lambda e: e.dma_start(out=gain_t[:], in_=gain[:, :]), writes=[b_gain])
    dma("sp", lambda e: e.dma_start(out=fgain_t[:], in_=fgain[:, :]), writes=[b_fgain])
    dma("sp", lambda e: e.dma_start(out=admB[:], in_=admBd[:, :]), writes=[b_admB])
    dma("sp", lambda e: e.dma_start(out=dsel[:], in_=dseld.rearrange("p (g q) -> p g q", g=8)), writes=[b_dsel])
    op("pool", lambda e: e.memset(neghalf[:], -0.5), writes=[b_neghalf])
    op("pool", lambda e: e.memset(vB[:], 1.0), writes=[b_vB])
    op("pool", lambda e: e.memset(vA[:], 1.0), writes=[b_vA])
    op("pool", lambda e: e.memset(kaT[:], 0.0), writes=[b_kaT])
    for t_, b_ in qaZ:
        op("pool", lambda e, t_=t_: e.memset(t_[:], 0.0), writes=[b_])
    op("pool", lambda e: e.memset(qbZ[:], 0.0), writes=[b_qbZ])
    op("pool", lambda e: e.memset(qiZ[:], 0.0), writes=[b_qiZ])

    stg = scores
    b_stg = b_scores
    op("pool", lambda e: e.memset(stg[:, 0:4096], 0.0), writes=[b_stg])
    dma("sp", lambda e: e.dma_start(out=stg[:, 0:128], in_=identd[:, :]), writes=[b_stg])
    op("dve", lambda e: e.tensor_copy(out=identb[:], in_=stg[:, 0:128]), reads=[b_stg], writes=[b_identb])
    op("dve", lambda e: e.tensor_copy(out=identf[:], in_=stg[:, 0:128]), reads=[b_stg], writes=[b_identf])
    dma("sp", lambda e: e.dma_start(out=stg[:, 0:6144], in_=biasAd[:, :]), writes=[b_stg])
    dma("sp", lambda e: e.dma_start(out=stg[:, 6144:6912], in_=maskAd[:, :]), writes=[b_stg])
    op("dve", lambda e: e.tensor_tensor(
        out=biasAm[:], in0=stg[:, 0:6144].rearrange("p (r h q) -> p r h q", r=6, h=8),
        in1=stg[:, 6144:6912].rearrange("p (r q) -> p r q", r=6).unsqueeze(2).to_broadcast([128, 6, 8, 128]),
        op=ALU.add), reads=[b_stg], writes=[b_biasAm])
    dma("sp", lambda e: e.dma_start(out=stg[:, 0:3072], in_=biasTd[:, :]), reads=[], writes=[b_stg])
    dma("sp", lambda e: e.dma_start(out=stg[:, 3072:3080], in_=cTd[:, :]), writes=[b_stg])
    op("dve", lambda e: e.tensor_tensor(
        out=biasTn[:], in0=stg[:, 0:3072].rearrange("p (r h q) -> p r h q", r=3, h=8),
        in1=stg[:, 3072:3080].unsqueeze(1).unsqueeze(3).to_broadcast([128, 3, 8, 128]),
        op=ALU.subtract), reads=[b_stg], writes=[b_biasTn])

    cvt = ring[0][0]
    b_cvt = ring[0][1]

    def cvt_chunk(si, pieces):
        for k in range(8):
            for (srcf, c0, ncl) in pieces:
                dma("sp", lambda e, k=k, srcf=srcf, c0=c0, ncl=ncl: e.dma_start(
                    out=stg[:, k * 512 + c0:k * 512 + c0 + ncl], in_=srcf(k)), writes=[b_stg])
        op("dve", lambda e: e.tensor_copy(out=cvt[:], in_=stg[:, 0:4096]), reads=[b_stg], writes=[b_cvt])
        dma("sp", lambda e, si=si: e.dma_start(out=scr[si][:, :], in_=cvt[:]), reads=[b_cvt], writes=[b_scr[si]])

    rows = lambda w_, k: slice(k * 128, (k + 1) * 128)
    cvt_chunk(0, [(lambda k: wkfm[k * 128:(k + 1) * 128, 0:512], 0, 512)])
    cvt_chunk(1, [(lambda k: wkfm[k * 128:(k + 1) * 128, 512:640], 0, 128),
                  (lambda k: wktm[k * 128:(k + 1) * 128, 512:576], 128, 64)])
    cvt_chunk(2, [(lambda k: wktm[k * 128:(k + 1) * 128, 0:512], 0, 512)])
    for g3 in range(3):
        cvt_chunk(3 + g3, [(lambda k, g3=g3: wqfm[k * 128:(k + 1) * 128, g3 * 512:(g3 + 1) * 512], 0, 512)])
    for ci in range(6):
        cvt_chunk(6 + ci, [(lambda k, ci=ci: wqtm[k * 128:(k + 1) * 128, ci * 512:(ci + 1) * 512], 0, 512)])
    for si, (src, r0) in enumerate([(wao, 0), (wbo, 0), (wo, 0), (wo, 512)]):
        for c4 in range(4):
            dma("sp", lambda e, c4=c4, src=src, r0=r0: e.dma_start(
                out=stg[:, c4 * 1024:(c4 + 1) * 1024], in_=src[r0 + c4 * 128:r0 + (c4 + 1) * 128, :]),
                writes=[b_stg])
        op("dve", lambda e: e.tensor_copy(out=cvt[:], in_=stg[:, 0:4096]), reads=[b_stg], writes=[b_cvt])
        dma("sp", lambda e, si=si: e.dma_start(out=scr[12 + si][:, :], in_=cvt[:]), reads=[b_cvt],
            writes=[b_scr[12 + si]])
    for k in range(8):
        dma("sp", lambda e, k=k: e.dma_start(out=stg[:, 0:8], in_=wwi[k * 128:(k + 1) * 128, :]), writes=[b_stg])
        op("dve", lambda e, k=k: e.tensor_copy(out=Wwi[:, k, :], in_=stg[:, 0:8]), reads=[b_stg], writes=[b_Wwi])

    ring_ctr = [0]

    def stream(si):
        r, b_r = ring[ring_ctr[0] % 2]
        ring_ctr[0] += 1
        dma("sp", lambda e: e.dma_start(out=r[:], in_=scr[si][:, :]), reads=[b_scr[si]], writes=[b_r])
        return r, b_r

    pair_ctr = [0]

    def next_pair():
        p = pair[pair_ctr[0] % 3]
        pair_ctr[0] += 1
        return p

    def rmsnorm_to(x_t, b_x, gain_tile, b_g, out_t, b_out):
        op("act", lambda e: e.activation(out=sqj[:], in_=x_t[:], func=AF.Square, accum_out=stat[:, 0:1]),
           reads=[b_x], writes=[b_sqj, b_stat])
        op("dve", lambda e: e.tensor_scalar(out=stat[:, 1:2], in0=stat[:, 0:1], scalar1=1.0 / D, scalar2=EPS,
                                            op0=ALU.mult, op1=ALU.add), reads=[b_stat], writes=[b_stat])
        op("pool", lambda e: e.tensor_tensor(out=stat[:, 2:3], in0=stat[:, 1:2], in1=neghalf[:], op=ALU.pow),
           reads=[b_stat, b_neghalf], writes=[b_stat])
        op("dve", lambda e: e.scalar_tensor_tensor(out=out_t[:], in0=x_t[:], scalar=stat[:, 2:3], in1=gain_tile[:],
                                                   op0=ALU.mult, op1=ALU.mult),
           reads=[b_x, b_stat, b_g], writes=[b_out])

    def norm_transpose(x_t, b_x, col):
        rmsnorm_to(x_t, b_x, gain_t, b_gain, hn, b_hn)
        for k in range(8):
            op("pe", lambda e, k=k: e.transpose(out=PB0[:, k, :], in_=hn[:, k * 128:(k + 1) * 128], identity=identb[:]),
               reads=[b_hn, b_identb], writes=[b_PB0])
        op("act", lambda e: e.activation(out=hnT[:, :, col:col + 128], in_=PB0[:], func=AF.Copy),
           reads=[b_PB0], writes=[b_hnT])

    idx_scale = (8 ** -0.5) * (64 ** -0.5)

    def slot(j):
        nk = 256 * (j + 1)
        pos0 = (2 * j) % 8
        for i in range(2):
            xt, b_xt = xin[i]
            t = 2 * j + i
            dma("sp", lambda e, xt=xt, t=t: e.dma_start(out=xt[:], in_=xf[t * 128:(t + 1) * 128, :]), writes=[b_xt])
            norm_transpose(xt, b_xt, i * 128)
        dma("sp", lambda e: e.dma_start(out=xown[:], in_=xo[j * 128:(j + 1) * 128, :]), writes=[b_xown])
        norm_transpose(xown, b_xown, 256)

        rA, b_rA = stream(0)
        for blk in range(4):
            pr, b_pr = next_pair()
            for k in range(8):
                op("pe", lambda e, k=k, blk=blk, pr=pr, rA=rA: e.matmul(
                    pr[:, 0:256], lhsT=rA[:, k * 512 + blk * 128:k * 512 + (blk + 1) * 128], rhs=hnT[:, k, 0:256],
                    start=(k == 0), stop=(k == 7)), reads=[b_rA, b_hnT], writes=[b_pr])
            op("act", lambda e, blk=blk, pr=pr: e.activation(
                out=kaT[:, blk, pos0 * 128:pos0 * 128 + 256], in_=pr[:, 0:256], func=AF.Copy),
                reads=[b_pr], writes=[b_kaT])
        rB, b_rB = stream(1)
        pr, b_pr = next_pair()
        for k in range(8):
            op("pe", lambda e, k=k, pr=pr, rB=rB: e.matmul(
                pr[:, 0:256], lhsT=rB[:, k * 512:k * 512 + 128], rhs=hnT[:, k, 0:256],
                start=(k == 0), stop=(k == 7)), reads=[b_rB, b_hnT], writes=[b_pr])
        op("act", lambda e, pr=pr: e.activation(
            out=kkT[:, 2 * j * 128:2 * j * 128 + 256], in_=pr[:, 0:256], func=AF.Copy),
            reads=[b_pr], writes=[b_kkT])
        for i in range(2):
            pr, b_pr = next_pair()
            for k in range(8):
                op("pe", lambda e, k=k, i=i, pr=pr, rB=rB: e.matmul(
                    pr[:, 0:64], lhsT=hnT[:, k, i * 128:(i + 1) * 128], rhs=rB[:, k * 512 + 128:k * 512 + 192],
                    start=(k == 0), stop=(k == 7)), reads=[b_rB, b_hnT], writes=[b_pr])
            op("dve", lambda e, i=i, pr=pr: e.tensor_copy(out=vB[:, 2 * j + i, 0:64], in_=pr[:, 0:64]),
               reads=[b_pr], writes=[b_vB])
        rC, b_rC = stream(2)
        for i in range(2):
            pr, b_pr = next_pair()
            for k in range(8):
                op("pe", lambda e, k=k, i=i, pr=pr, rC=rC: e.matmul(
                    pr[:, 0:512], lhsT=hnT[:, k, i * 128:(i + 1) * 128], rhs=rC[:, k * 512:(k + 1) * 512],
                    start=(k == 0), stop=(k == 7)), reads=[b_rC, b_hnT], writes=[b_pr])
            op("dve", lambda e, i=i, pr=pr: e.tensor_copy(
                out=vA[:, pos0 + i, :, 0:64], in_=pr[:, 0:512].rearrange("p (h d) -> p h d", h=8)),
                reads=[b_pr], writes=[b_vA])

        for grp in range(3):
            rQ, b_rQ = stream(3 + grp)
            pr, b_pr = next_pair()
            for b4 in range(4):
                blk = grp * 4 + b4
                for k in range(8):
                    op("pe", lambda e, k=k, b4=b4, pr=pr, rQ=rQ: e.matmul(
                        pr[:, b4 * 128:(b4 + 1) * 128], lhsT=rQ[:, k * 512 + b4 * 128:k * 512 + (b4 + 1) * 128],
                        rhs=hnT[:, k, 256:384], start=(k == 0), stop=(k == 7)),
                        reads=[b_rQ, b_hnT], writes=[b_pr])
            src = lambda lo, hi, pr=pr: pr[lo:hi, 0:512].rearrange("p (b q) -> p b q", b=4)
            if grp == 0:
                op("act", lambda e, src=src: e.activation(out=qaZ[0][0][0:64, :, :], in_=src(0, 64), func=AF.Copy,
                                                          scale=0.125), reads=[b_pr], writes=[qaZ[0][1]])
                op("act", lambda e, src=src: e.activation(out=qaZ[1][0][64:128, :, :], in_=src(64, 128), func=AF.Copy,
                                                          scale=0.125), reads=[b_pr], writes=[qaZ[1][1]])
            else:
                h0 = (grp - 1) * 4
                op("act", lambda e, src=src, h0=h0: e.activation(out=qbZ[0:64, h0:h0 + 4, :], in_=src(0, 64),
                                                                 func=AF.Copy, scale=0.125),
                   reads=[b_pr], writes=[b_qbZ])
                op("act", lambda e, pr=pr, h0=h0: e.activation(
                    out=qiZ[64:128, :, h0:h0 + 4, :],
                    in_=pr[64:128, 0:512].rearrange("p (h g q) -> p g h q", h=4, g=8), func=AF.Copy),
                    reads=[b_pr], writes=[b_qiZ])
        pr, b_pr = next_pair()
        for k in range(8):
            op("pe", lambda e, k=k, pr=pr: e.matmul(pr[:, 0:8], lhsT=hnT[:, k, 256:384], rhs=Wwi[:, k, :],
                                                    start=(k == 0), stop=(k == 7)),
               reads=[b_Wwi, b_hnT], writes=[b_pr])
        op("act", lambda e, pr=pr: e.activation(out=wsc[:], in_=pr[:, 0:8], func=AF.Copy, scale=idx_scale),
           reads=[b_pr], writes=[b_wsc])
        op("dve", lambda e: e.tensor_tensor(
            out=Tb[:], in0=wsc[:].unsqueeze(1).unsqueeze(3).to_broadcast([128, 8, 8, 16]),
            in1=dsel[:].unsqueeze(2).to_broadcast([128, 8, 8, 16]), op=ALU.mult),
            reads=[b_wsc, b_dsel], writes=[b_Tb])
        for g in range(8):
            op("pe", lambda e, g=g: e.transpose(out=PB0[:, g, :], in_=Tb[:, g, :, :].rearrange("p h q -> p (h q)"),
                                                identity=identb[:]), reads=[b_Tb, b_identb], writes=[b_PB0])
        op("act", lambda e: e.activation(out=Wblk[:], in_=PB0[:], func=AF.Copy), reads=[b_PB0], writes=[b_Wblk])

        for ci in range(6):
            r, b_r = stream(6 + ci)
            pr, b_pr = next_pair()
            for k in range(8):
                op("pe", lambda e, k=k, r=r, pr=pr: e.matmul(
                    pr[:, 0:512], lhsT=hnT[:, k, 256:384], rhs=r[:, k * 512:(k + 1) * 512],
                    start=(k == 0), stop=(k == 7)), reads=[b_r, b_hnT], writes=[b_pr])
            op("act", lambda e, pr=pr: e.activation(out=tmpa[:], in_=pr[:, 0:512], func=AF.Tanh, scale=0.5),
               reads=[b_pr], writes=[b_tmpa])
            if ci < 2:
                dst, b_dst = (sza, b_sza) if ci == 0 else (szb, b_szb)
                op("dve", lambda e: e.tensor_scalar(out=tmpb[:], in0=tmpa[:], scalar1=0.5, scalar2=0.5,
                                                    op0=ALU.mult, op1=ALU.add), reads=[b_tmpa], writes=[b_tmpb])
                op("dve", lambda e, dst=dst, pr=pr: e.tensor_tensor(out=dst[:], in0=tmpb[:], in1=pr[:, 0:512],
                                                                    op=ALU.mult),
                   reads=[b_tmpb, b_pr], writes=[b_dst])
            else:
                dst, b_dst = (sga, b_sga) if ci < 4 else (sgb, b_sgb)
                c0 = (ci % 2) * 512
                op("dve", lambda e, dst=dst, c0=c0: e.tensor_scalar(
                    out=dst[:, c0:c0 + 512], in0=tmpa[:], scalar1=0.5, scalar2=0.5, op0=ALU.mult, op1=ALU.add),
                    reads=[b_tmpa], writes=[b_dst])

        rvalid = [r for r in range(6) if 2 * j - 4 + r >= 0]
        nr = len(rvalid)
        pv, b_pv = pair[2]

        def a_front(h):
            p, e2 = h // 2, h % 2
            sp_, b_sp = pair[h % 2]
            for idx, r in enumerate(rvalid):
                kp = (2 * j - 4 + r) % 8
                op("pe", lambda e, idx=idx, p=p, kp=kp, e2=e2, sp_=sp_: e.matmul(
                    sp_[:, idx * 128:(idx + 1) * 128], lhsT=kaT[:, p, kp * 128:(kp + 1) * 128],
                    rhs=qaZ[e2][0][:, p, :], start=True, stop=False),
                    reads=[b_kaT, qaZ[e2][1]], writes=[b_sp])
                op("pe", lambda e, idx=idx, r=r, h=h, sp_=sp_: e.matmul(
                    sp_[:, idx * 128:(idx + 1) * 128], lhsT=identb[:], rhs=biasAm[:, r, h, :],
                    start=False, stop=True), reads=[b_identb, b_biasAm], writes=[b_sp])

        def a_back(h):
            sp_, b_sp = pair[h % 2]
            pa, b_pa = PAs[h % 2]
            op("act", lambda e, sp_=sp_, pa=pa: e.activation(
                out=pa[:, 0:nr, :].rearrange("p r q -> p (r q)"), in_=sp_[:, 0:nr * 128], func=AF.Exp),
                reads=[b_sp], writes=[b_pa])
            for idx, r in enumerate(rvalid):
                kp = (2 * j - 4 + r) % 8
                op("pe", lambda e, idx=idx, kp=kp, h=h, pa=pa: e.matmul(
                    pv[:, h * 128:h * 128 + 65], lhsT=pa[:, idx, :], rhs=vA[:, kp, h, 0:65],
                    start=(idx == 0), stop=(idx == nr - 1)), reads=[b_pa, b_vA], writes=[b_pv])

        a_front(0)
        for h in range(8):
            if h + 1 < 8:
                a_front(h + 1)
            a_back(h)

        def finish_branch(pv, b_pv, sz, b_sz, yg, b_yg):
            pv3 = pv[:, :].rearrange("p (h c) -> p h c", h=8)
            op("dve", lambda e: e.reciprocal(out=rden[:], in_=pv3[:, :, 64]), reads=[b_pv], writes=[b_rden])
            op("dve", lambda e: e.tensor_tensor(out=yun[:], in0=pv3[:, :, 0:64],
                                                in1=rden[:].unsqueeze(2).to_broadcast([128, 8, 64]), op=ALU.mult),
               reads=[b_pv, b_rden], writes=[b_yun])
            op("dve", lambda e: e.tensor_tensor(out=yg[:], in0=yun[:].rearrange("p h d -> p (h d)"), in1=sz[:],
                                                op=ALU.mult), reads=[b_yun, b_sz], writes=[b_yg])

        finish_branch(pv, b_pv, sza, b_sza, yag, b_yag)

        nblk = (nk + 511) // 512
        items = [(blk, g) for blk in range(nblk) for g in range(8)]
        sacc, b_sacc = pair[2]

        def i_front(n):
            blk, g = items[n]
            w = min(512, nk - 512 * blk)
            k0 = blk * 512
            lg, b_lg = pair[n % 2]
            op("pe", lambda e, g=g, w=w, k0=k0, lg=lg: e.matmul(
                lg[:, 0:w], lhsT=qiZ[:, g, :, :].rearrange("p h q -> p (h q)"), rhs=kkT[:, k0:k0 + w],
                start=True, stop=True), reads=[b_qiZ, b_kkT], writes=[b_lg])

        def i_back(n):
            blk, g = items[n]
            w = min(512, nk - 512 * blk)
            k0 = blk * 512
            lg, b_lg = pair[n % 2]
            R_, b_R = Rr[n % 3]
            op("act", lambda e, w=w, lg=lg, R_=R_: e.activation(out=R_[:, 0:w], in_=lg[:, 0:w], func=AF.Relu),
               reads=[b_lg], writes=[b_R])
            op("pe", lambda e, g=g, w=w, R_=R_: e.matmul(
                sacc[:, 0:w], lhsT=Wblk[:, g, :], rhs=R_[:, 0:w], start=(g == 0), stop=(g == 7)),
                reads=[b_Wblk, b_R], writes=[b_sacc])
            if g == 7:
                last = (blk == nblk - 1)
                wcopy = w - 256 if last else w
                if wcopy > 0:
                    op("act", lambda e, k0=k0, wcopy=wcopy: e.activation(
                        out=scores[:, k0:k0 + wcopy], in_=sacc[:, 0:wcopy], func=AF.Copy),
                        reads=[b_sacc], writes=[b_scores])
                if last:
                    op("dve", lambda e, w=w: e.tensor_tensor(
                        out=scores[:, nk - 256:nk], in0=sacc[:, w - 256:w], in1=admB[:], op=ALU.add),
                        reads=[b_sacc, b_admB], writes=[b_scores])

        i_front(0)
        for n in range(len(items)):
            if n + 1 < len(items):
                i_front(n + 1)
            i_back(n)

        hA = 128 * int(round(0.55 * nk / 128)) if nk >= 1024 else 0
        hd = nk - hA
        op("dve", lambda e: e.memset(cand[0][0][:], 0.0), writes=[cand[0][1]])
        if hA:
            op("dve", lambda e: e.memset(ncand[0][0][:], 0.0), writes=[ncand[0][1]])
        step = BIS_M
        for it in range(NBIS):
            ca, b_ca = cand[it % 2]
            cb, b_cb = cand[(it + 1) % 2]
            nca, b_nca = ncand[it % 2]
            ncb, b_ncb = ncand[(it + 1) % 2]
            if hA:
                op("act", lambda e, nca=nca: e.activation(
                    out=maskf[:, hd:nk], in_=scores[:, hd:nk], func=AF.Sign, bias=nca[:, 0:1], scale=1.0,
                    accum_out=cntA[:, 0:1]), reads=[b_scores, b_nca], writes=[b_maskf_hi, b_cntA])
            op("dve", lambda e, ca=ca: e.tensor_scalar(
                out=maskf[:, 0:hd], in0=scores[:, 0:hd], scalar1=ca[:, 0:1], scalar2=None, op0=ALU.is_ge,
                op1=ALU.add, accum_out=cnt[:, 0:1]), reads=[b_scores, b_ca], writes=[b_maskf, b_cnt])
            if hA:
                op("dve", lambda e: e.scalar_tensor_tensor(out=cnt[:], in0=cntA[:], scalar=0.5, in1=cnt[:],
                                                           op0=ALU.mult, op1=ALU.add),
                   reads=[b_cntA, b_cnt], writes=[b_cnt])
            op("dve", lambda e: e.tensor_scalar(out=pmh[:], in0=cnt[:], scalar1=float(nsel) - hA / 2.0, scalar2=0.5,
                                                op0=ALU.is_ge, op1=ALU.subtract), reads=[b_cnt], writes=[b_pmh])
            op("dve", lambda e, ca=ca, cb=cb, step=step: e.scalar_tensor_tensor(
                out=cb[:], in0=pmh[:], scalar=float(step), in1=ca[:], op0=ALU.mult, op1=ALU.add),
                reads=[b_pmh, b_ca], writes=[b_cb])
            if hA:
                op("dve", lambda e, nca=nca, ncb=ncb, step=step: e.scalar_tensor_tensor(
                    out=ncb[:], in0=pmh[:], scalar=-float(step), in1=nca[:], op0=ALU.mult, op1=ALU.add),
                    reads=[b_pmh, b_nca], writes=[b_ncb])
            step = step / 2.0
        cf, b_cf = cand[NBIS % 2]
        op("dve", lambda e, cf=cf, step=step: e.tensor_scalar(out=thr[:], in0=cf[:], scalar1=-float(step),
                                                              scalar2=None, op0=ALU.add),
           reads=[b_cf], writes=[b_thr])
        op("dve", lambda e: e.tensor_scalar(out=maskf[:, 0:nk], in0=scores[:, 0:nk], scalar1=thr[:, 0:1],
                                            scalar2=None, op0=ALU.is_ge),
           reads=[b_scores, b_thr], writes=[b_maskf, b_maskf_hi])

        pvb, b_pvb = pair[2]
        nkt = 2 * j + 2

        def b_front(kt):
            s4 = kt % 4
            sp_, b_sp = pair[kt % 2]
            near = kt >= 2 * j - 1
            for half in range(2):
                op("pe", lambda e, kt=kt, half=half, near=near, sp_=sp_: e.matmul(
                    sp_[:, half * 512:(half + 1) * 512], lhsT=kkT[:, kt * 128:(kt + 1) * 128],
                    rhs=qbZ[:, 4 * half:4 * half + 4, :].rearrange("p h q -> p (h q)"), start=True, stop=(not near)),
                    reads=[b_kkT, b_qbZ], writes=[b_sp])
                if near:
                    r = kt - (2 * j - 1)
                    op("pe", lambda e, half=half, r=r, sp_=sp_: e.matmul(
                        sp_[:, half * 512:(half + 1) * 512], lhsT=identb[:],
                        rhs=biasTn[:, r, 4 * half:4 * half + 4, :].rearrange("p h q -> p (h q)"), start=False, stop=True),
                        reads=[b_identb, b_biasTn], writes=[b_sp])
            op("pe", lambda e, kt=kt, s4=s4: e.transpose(out=PB1[:, s4, :], in_=maskf[:, kt * 128:(kt + 1) * 128],
                                                         identity=identb[:]),
               reads=[b_maskf, b_maskf_hi, b_identb], writes=[b_PB1])

        def b_back(kt):
            s4 = kt % 4
            sp_, b_sp = pair[kt % 2]
            pt, b_pt = PT[kt % 2]
            ptm, b_ptm = PTm[kt % 2]
            op("act", lambda e, sp_=sp_, pt=pt: e.activation(out=pt[:].rearrange("p h q -> p (h q)"),
                                                             in_=sp_[:, :], func=AF.Exp),
               reads=[b_sp], writes=[b_pt])
            op("dve", lambda e, s4=s4, pt=pt, ptm=ptm: e.tensor_tensor(
                out=ptm[:], in0=pt[:], in1=PB1[:, s4, :].unsqueeze(1).to_broadcast([128, 8, 128]), op=ALU.mult),
                reads=[b_pt, b_PB1], writes=[b_ptm])
            for half in range(2):
                op("pe", lambda e, kt=kt, half=half, ptm=ptm: e.matmul(
                    pvb[0:65, half * 512:(half + 1) * 512], lhsT=vB[:, kt, 0:65],
                    rhs=ptm[:, 4 * half:4 * half + 4, :].rearrange("p h q -> p (h q)"),
                    start=(kt == 0), stop=(kt == nkt - 1)), reads=[b_ptm, b_vB], writes=[b_pvb])

        b_front(0)
        for kt in range(nkt):
            if kt + 1 < nkt:
                b_front(kt + 1)
            b_back(kt)
        op("act", lambda e: e.activation(out=hres[0:65, :], in_=pvb[0:65, :], func=AF.Copy),
           reads=[b_pvb], writes=[b_hres])
        pvt, b_pvt = pair[0]
        for h in range(8):
            op("pe", lambda e, h=h: e.transpose(out=pvt[:, h * 128:h * 128 + 65], in_=hres[0:65, h * 128:(h + 1) * 128],
                                                identity=identf[0:65, 0:65]),
               reads=[b_hres, b_identf], writes=[b_pvt])
        finish_branch(pvt, b_pvt, szb, b_szb, ybg, b_ybg)
        pair_ctr[0] = 0

        def out_proj_branch(yg, b_yg, si, sg, b_sg, first):
            for c4 in range(4):
                op("pe", lambda e, c4=c4: e.transpose(out=PB0[:, c4, :], in_=yg[:, c4 * 128:(c4 + 1) * 128],
                                                      identity=identb[:]), reads=[b_yg, b_identb], writes=[b_PB0])
            op("act", lambda e: e.activation(out=yT[:], in_=PB0[:, 0:4, :], func=AF.Copy), reads=[b_PB0], writes=[b_yT])
            r, b_r = stream(si)
            pr, b_pr = next_pair()
            for half in range(2):
                for c4 in range(4):
                    op("pe", lambda e, c4=c4, half=half, r=r, pr=pr: e.matmul(
                        pr[:, half * 512:(half + 1) * 512], lhsT=yT[:, c4, :],
                        rhs=r[:, c4 * 1024 + half * 512:c4 * 1024 + (half + 1) * 512],
                        start=(c4 == 0), stop=(c4 == 3)), reads=[b_yT, b_r], writes=[b_pr])
            if first:
                op("dve", lambda e, pr=pr: e.tensor_tensor(out=merged[:], in0=pr[:, :], in1=sg[:], op=ALU.mult),
                   reads=[b_pr, b_sg], writes=[b_merged])
            else:
                op("dve", lambda e, pr=pr: e.tensor_tensor(out=hres[:], in0=pr[:, :], in1=sg[:], op=ALU.mult),
                   reads=[b_pr, b_sg], writes=[b_hres])
                op("dve", lambda e: e.tensor_tensor(out=mergedb[:], in0=merged[:], in1=hres[:], op=ALU.add),
                   reads=[b_merged, b_hres], writes=[b_mergedb])

        out_proj_branch(yag, b_yag, 12, sga, b_sga, True)
        out_proj_branch(ybg, b_ybg, 13, sgb, b_sgb, False)
        for k in range(8):
            op("pe", lambda e, k=k: e.transpose(out=PB0[:, k, :], in_=mergedb[:, k * 128:(k + 1) * 128],
                                                identity=identb[:]), reads=[b_mergedb, b_identb], writes=[b_PB0])
        op("act", lambda e: e.activation(out=mT[:], in_=PB0[:], func=AF.Copy), reads=[b_PB0], writes=[b_mT])
        pr, b_pr = next_pair()
        for wi_ in range(2):
            r, b_r = stream(14 + wi_)
            for half in range(2):
                for c4 in range(4):
                    k = wi_ * 4 + c4
                    op("pe", lambda e, c4=c4, k=k, half=half, r=r, pr=pr: e.matmul(
                        pr[:, half * 512:(half + 1) * 512], lhsT=mT[:, k, :],
                        rhs=r[:, c4 * 1024 + half * 512:c4 * 1024 + (half + 1) * 512],
                        start=(k == 0), stop=(k == 7), skip_group_check=True), reads=[b_mT, b_r], writes=[b_pr])
        op("dve", lambda e, pr=pr: e.tensor_tensor(out=hres[:], in0=pr[:, :], in1=xown[:], op=ALU.add),
           reads=[b_pr, b_xown], writes=[b_hres])
        rmsnorm_to(hres, b_hres, fgain_t, b_fgain, yout, b_yout)
        b_o = Buf("o%d" % j)
        dma("sp", lambda e: e.dma_start(out=outd[j * 128:(j + 1) * 128, :], in_=yout[:]), reads=[b_yout], writes=[b_o])
        obufs.append(b_o)
    for j_ in range(NS):
        slot(j_)
    sch.final_wait("sp", obufs)
    sch.emit(nc)
    st.close()
    return nc


def _t5_bucket(rel):
    half, max_exact = 16, 8
    ret = np.where(rel > 0, half, 0)
    n = np.abs(rel)
    nf = np.maximum(n, 1).astype(np.float32)
    large = max_exact + (np.log(nf / np.float32(max_exact)) / np.float32(math.log(128 / max_exact))
                         * np.float32(half - max_exact)).astype(np.int32)
    large = np.minimum(large, half - 1)
    return ret + np.where(n < max_exact, n, large)


def prep_shared(norm_gain, w_in, w_a_out, w_b_out, w_out, final_gain):
    W = np.asarray(w_in[0])
    offs = np.cumsum([0, 512, 512, 512, 512, 512, 64, 64, 512, 512, 64, 8, 1024, 1024])
    names = ["qa", "ka", "va", "za", "qb", "kb", "vb", "zb", "qi", "ki", "wi", "ga", "gb"]
    col = {n: np.arange(offs[i], offs[i + 1]) for i, n in enumerate(names)}
    qbqi = np.concatenate([np.concatenate([col["qb"][h * 64:(h + 1) * 64], col["qi"][h * 64:(h + 1) * 64]])
                           for h in range(8)])
    sh = {
        "wkfm": np.ascontiguousarray(W[:, np.concatenate([col["ka"], col["kb"], col["ki"]])]),
        "wktm": np.ascontiguousarray(W[:, np.concatenate([col["va"], col["vb"]])]),
        "wqfm": np.ascontiguousarray(W[:, np.concatenate([col["qa"], qbqi])]),
        "wqtm": np.ascontiguousarray(W[:, np.concatenate([col["za"], col["zb"], col["ga"], col["gb"]])]),
        "wwi": np.ascontiguousarray(W[:, col["wi"]]),
        "wao": np.ascontiguousarray(w_a_out[0]),
        "wbo": np.ascontiguousarray(w_b_out[0]),
        "wo": np.ascontiguousarray(w_out[0]),
        "gain": np.ascontiguousarray(np.broadcast_to(np.asarray(norm_gain[0])[None, :], (128, D))),
        "fgain": np.ascontiguousarray(np.broadcast_to(np.asarray(final_gain)[None, :], (128, D))),
        "ident": np.eye(128, dtype=np.float32),
    }
    ds = np.zeros((128, 8, 16), np.float32)
    for q in range(128):
        ds[q, q // 16, q % 16] = 1.0
    sh["dsel"] = ds.reshape(128, 128)
    return sh


def prep_core_consts(c, a_rel_bias, t5_bias):
    arb = np.asarray(a_rel_bias[0])
    tb = np.asarray(t5_bias)
    so = np.arange(128)[:, None]
    to = np.arange(128)[None, :]
    biasA = np.zeros((128, 6, 8, 128), np.float32)
    maskA = np.zeros((128, 6, 128), np.float32)
    for r in range(6):
        rel = (c + 4 - r) * 128 + to - so
        idx = np.clip(rel, -256, 256) + 256
        biasA[:, r, :, :] = np.transpose(arb[:, idx], (1, 0, 2))
        diff = 2 * c + 8 - 2 * r + (to >= 64).astype(int) - (so >= 64).astype(int)
        maskA[:, r, :] = np.where((diff >= 0) & (diff <= 8), 0.0, -30000.0)
    biasT = np.zeros((128, 3, 8, 128), np.float32)
    for r in range(3):
        rel = (r - 1 - c) * 128 + so - to
        bk = _t5_bucket(rel.astype(np.int32))
        biasT[:, r, :, :] = np.transpose(tb[bk], (0, 2, 1))
    cT = np.ascontiguousarray(np.broadcast_to(tb[15][None, :], (128, 8)))
    adm = np.zeros((128, 256), np.float32)
    qh = (np.arange(128) >= 64).astype(int)[:, None]
    kc4 = (np.arange(256) // 64)[None, :]
    adm[:] = np.where(kc4 <= 2 * c + qh, 0.0, -1e30)
    return {"biasA": biasA.reshape(128, -1), "maskA": maskA.reshape(128, -1),
            "biasT": biasT.reshape(128, -1), "cT": cT, "admB": adm}


def make_in_maps(x, norm_gain, w_in, a_rel_bias, t5_bias, w_a_out, w_b_out, w_out, final_gain):
    x = np.asarray(x)
    B, S, _ = x.shape
    sh = prep_shared(norm_gain, w_in, w_a_out, w_b_out, w_out, final_gain)
    cc = [prep_core_consts(c, a_rel_bias, t5_bias) for c in range(2)]
    maps = []
    for b in range(B):
        xb = x[b].reshape(S // 128, 128, D)
        for c in range(2):
            m = dict(sh)
            m.update(cc[c])
            m["xf"] = np.ascontiguousarray(x[b])
            m["xo"] = np.ascontiguousarray(xb[c::2].reshape(S // 2, D))
            maps.append(m)
    return maps


def assemble(results, B, S):
    out = np.zeros((B, S // 128, 128, D), np.float32)
    i = 0
    for b in range(B):
        for c in range(2):
            out[b, c::2] = np.asarray(results[i]["out"]).reshape(S // 256, 128, D)
            i += 1
    return out.reshape(B, S, D)


def kernel(x, norm_gain, w_in, a_rel_bias, t5_bias, w_a_out, w_b_out, w_out, final_gain):
    x = np.asarray(x)
    B, S, _ = x.shape
    nc = build(S, min(256, S // 4))
    maps = make_in_maps(x, norm_gain, w_in, a_rel_bias, t5_bias, w_a_out, w_b_out, w_out, final_gain)
    res = run_bass_kernel_spmd(nc, maps, core_ids=list(range(len(maps))))
    return assemble(res.results, B, S)
```
